# Optimizing a Trainium2 kernel written in Bass

```python
import math
import jax, jax.numpy as jnp
from jax import lax
import numpy as np

D_MODEL = 4096
BATCH = 8
SEQ = 2048
DEPTH = 2

HEAD_DIM = 128
DIL_WINDOWS = (128, 512, 2048)
DIL_RATES = (1, 4, 16)
N_DIL_GROUPS = 3
DIL_HEADS = D_MODEL // HEAD_DIM // 2
DIL_WIDTH = DIL_HEADS * HEAD_DIM
A_IN_COLS = N_DIL_GROUPS * 3 * DIL_WIDTH
FOX_HEADS = D_MODEL // HEAD_DIM
FOX_WIDTH = FOX_HEADS * HEAD_DIM
B_IN_COLS = 3 * FOX_WIDTH + FOX_HEADS
Q_BLOCK = 128
PEER_HEADS = 8
PEER_NKEYS = 128
PEER_EXPERTS = PEER_NKEYS * PEER_NKEYS
PEER_QDIM = 256
PEER_TOPK = 16
PEER_CHUNK = 128
DN_ALPHA = (2 * DEPTH) ** 0.25
DN_BETA = (8 * DEPTH) ** -0.25
LN_EPS = 1e-5
N_A_LAYERS = (DEPTH + 1) // 2
N_B_LAYERS = DEPTH // 2
NEG = -1e30

kernel_name = "hybrid_dilated_fox_peer_deepnorm"


def alibi_slopes(n_heads):
    return jnp.exp2(-8.0 * (jnp.arange(n_heads, dtype=jnp.float32) + 1.0) / n_heads)


def layer_norm(x, g, b):
    xf = x.astype(jnp.float32)
    mu = jnp.mean(xf, axis=-1, keepdims=True)
    var = jnp.mean(jnp.square(xf - mu), axis=-1, keepdims=True)
    return ((xf - mu) * lax.rsqrt(var + LN_EPS) * g.astype(jnp.float32) + b.astype(jnp.float32)).astype(x.dtype)


def dilated_window_attention(q, k, v, dil, window, slopes):
    B, S, H, Dh = q.shape
    n = window // dil
    L = S // dil
    nb = -(-L // n)
    Lp = nb * n

    def strided(t):
        t = t.reshape(B, L, dil, H, Dh).transpose(0, 2, 3, 1, 4)
        return jnp.pad(t, ((0, 0), (0, 0), (0, 0), (0, Lp - L), (0, 0)))

    def banded(t):
        t = jnp.pad(t, ((0, 0), (0, 0), (0, 0), (n, 0), (0, 0))).reshape(B, dil, H, nb + 1, n, Dh)
        return jnp.concatenate([t[:, :, :, :-1], t[:, :, :, 1:]], axis=4)

    qb = strided(q).reshape(B, dil, H, nb, n, Dh)
    kb = banded(strided(k))
    vb = banded(strided(v))
    logits = jnp.einsum('brhnqc,brhnkc->brhnqk', qb, kb).astype(jnp.float32) / math.sqrt(Dh)
    i = jnp.arange(n)[:, None]
    j = jnp.arange(2 * n)[None, :]
    dist = n + i - j
    blk = jnp.arange(nb)[:, None, None]
    valid = (dist >= 0) & (dist <= n) & (blk * n + j - n >= 0)
    bias = -slopes[:, None, None, None] * (dil * dist).astype(jnp.float32)[None, None]
    logits = jnp.where(valid, logits + bias, NEG)
    mx = jnp.max(logits, axis=-1, keepdims=True)
    p = jnp.exp(logits - mx)
    den = jnp.sum(p, axis=-1, keepdims=True)
    o = jnp.einsum('brhnqk,brhnkc->brhnqc', p, vb.astype(jnp.float32)) / den
    lse = (mx + jnp.log(den))[..., 0]
    o = o.reshape(B, dil, H, Lp, Dh)[:, :, :, :L].transpose(0, 3, 1, 2, 4).reshape(B, S, H, Dh)
    lse = lse.reshape(B, dil, H, Lp)[:, :, :, :L].transpose(0, 3, 1, 2).reshape(B, S, H)
    return o, lse


def dilated_mixer(x, w_in, w_out):
    B, S, _ = x.shape
    proj = (x @ w_in).reshape(B, S, N_DIL_GROUPS, 3, DIL_HEADS, HEAD_DIM)
    slopes = alibi_slopes(DIL_HEADS)
    outs, lses = [], []
    for g in range(N_DIL_GROUPS):
        o, lse = dilated_window_attention(proj[:, :, g, 0], proj[:, :, g, 1], proj[:, :, g, 2],
                                          DIL_RATES[g], DIL_WINDOWS[g], slopes)
        outs.append(o)
        lses.append(lse)
    wgt = jax.nn.softmax(jnp.stack(lses), axis=0)
    o = jnp.einsum('gbsh,gbshc->bshc', wgt, jnp.stack(outs))
    return o.reshape(B, S, DIL_WIDTH).astype(x.dtype) @ w_out


def forgetting_mixer(x, w_in, f_bias, w_out):
    B, S, _ = x.shape
    proj = x @ w_in
    qkv = proj[..., :3 * FOX_WIDTH].reshape(B, S, 3, FOX_HEADS, HEAD_DIM)
    q = qkv[:, :, 0].transpose(0, 2, 1, 3)
    k = qkv[:, :, 1].transpose(0, 2, 1, 3)
    v = qkv[:, :, 2].transpose(0, 2, 1, 3).astype(jnp.float32)
    log_f = jax.nn.log_sigmoid((proj[..., 3 * FOX_WIDTH:] + f_bias).astype(jnp.float32))
    c = jnp.cumsum(log_f, axis=1).transpose(0, 2, 1)
    nb = S // Q_BLOCK
    qb = q.reshape(B, FOX_HEADS, nb, Q_BLOCK, HEAD_DIM).transpose(2, 0, 1, 3, 4)
    cb = c.reshape(B, FOX_HEADS, nb, Q_BLOCK).transpose(2, 0, 1, 3)
    kpos = jnp.arange(S)
    scale = 1.0 / math.sqrt(HEAD_DIM)

    def block(args):
        qi, ci, start = args
        logits = jnp.einsum('bhqc,bhkc->bhqk', qi, k).astype(jnp.float32) * scale
        logits = logits + ci[..., None] - c[:, :, None, :]
        qpos = start + jnp.arange(Q_BLOCK)
        logits = jnp.where(kpos[None, :] <= qpos[:, None], logits, NEG)
        p = jax.nn.softmax(logits, axis=-1)
        return jnp.einsum('bhqk,bhkc->bhqc', p, v)

    o = lax.map(block, (qb, cb, jnp.arange(nb) * Q_BLOCK))
    o = o.transpose(1, 0, 3, 2, 4).reshape(B, S, FOX_WIDTH)
    return o.astype(x.dtype) @ w_out


def peer_ffn(x, w_q, sub_keys, u, v):
    B, S, D = x.shape
    T = B * S
    xt = x.reshape(T, D)
    q = (xt @ w_q).reshape(T, PEER_HEADS, 2, PEER_QDIM // 2)
    s = jnp.einsum('thpc,pnc->thpn', q, sub_keys).astype(jnp.float32)
    sv, si = lax.top_k(s, PEER_TOPK)
    cand = (sv[:, :, 0, :, None] + sv[:, :, 1, None, :]).reshape(T, PEER_HEADS, PEER_TOPK * PEER_TOPK)
    cidx = (si[:, :, 0, :, None] * PEER_NKEYS + si[:, :, 1, None, :]).reshape(T, PEER_HEADS, PEER_TOPK * PEER_TOPK)
    top_s, pos = lax.top_k(cand, PEER_TOPK)
    eidx = jnp.take_along_axis(cidx, pos, axis=-1)
    gate = jax.nn.softmax(top_s, axis=-1)
    nc = T // PEER_CHUNK

    def chunk(args):
        xc, ec, gc = args
        act = jax.nn.gelu(jnp.einsum('cd,chkd->chk', xc, u[ec]).astype(jnp.float32), approximate=False)
        return jnp.einsum('chk,chkd->cd', (gc * act).astype(x.dtype), v[ec])

    out = lax.map(chunk, (xt.reshape(nc, PEER_CHUNK, D),
                          eidx.reshape(nc, PEER_CHUNK, PEER_HEADS, PEER_TOPK),
                          gate.reshape(nc, PEER_CHUNK, PEER_HEADS, PEER_TOPK)))
    return out.reshape(B, S, D)


def setup_inputs(seed: int = 0) -> dict:
    key = jax.random.key(seed)
    ks = jax.random.split(key, 20)
    D = D_MODEL
    nrm = lambda k, shape, scale: jax.random.normal(k, shape, jnp.float32) * scale
    x = nrm(ks[0], (BATCH, SEQ, D), 1.0)
    a_qk = nrm(ks[1], (N_A_LAYERS, D, N_DIL_GROUPS, 2, DIL_WIDTH), D ** -0.5)
    a_v = nrm(ks[2], (N_A_LAYERS, D, N_DIL_GROUPS, 1, DIL_WIDTH), D ** -0.5 * DN_BETA)
    a_w_in = jnp.concatenate([a_qk, a_v], axis=3).reshape(N_A_LAYERS, D, A_IN_COLS)
    a_w_out = nrm(ks[3], (N_A_LAYERS, DIL_WIDTH, D), DIL_WIDTH ** -0.5 * DN_BETA)
    b_qk = nrm(ks[4], (N_B_LAYERS, D, 2 * FOX_WIDTH), D ** -0.5)
    b_v = nrm(ks[5], (N_B_LAYERS, D, FOX_WIDTH), D ** -0.5 * DN_BETA)
    b_fw = nrm(ks[6], (N_B_LAYERS, D, FOX_HEADS), D ** -0.5)
    b_w_in = jnp.concatenate([b_qk, b_v, b_fw], axis=-1)
    b_f_bias = 1.0 + 2.0 * jax.random.uniform(ks[7], (N_B_LAYERS, FOX_HEADS), jnp.float32)
    b_w_out = nrm(ks[8], (N_B_LAYERS, FOX_WIDTH, D), FOX_WIDTH ** -0.5 * DN_BETA)
    peer_w_q = nrm(ks[9], (DEPTH, D, PEER_HEADS * PEER_QDIM), D ** -0.5)
    peer_sub_keys = nrm(ks[10], (DEPTH, 2, PEER_NKEYS, PEER_QDIM // 2), (PEER_QDIM // 2) ** -0.5)
    peer_u = nrm(ks[11], (DEPTH, PEER_EXPERTS, D), D ** -0.5)
    peer_v = nrm(ks[12], (DEPTH, PEER_EXPERTS, D), DN_BETA * PEER_HEADS ** -0.5)
    ln_mix_g = 1.0 + nrm(ks[13], (DEPTH, D), 0.02)
    ln_mix_b = nrm(ks[14], (DEPTH, D), 0.02)
    ln_ffn_g = 1.0 + nrm(ks[15], (DEPTH, D), 0.02)
    ln_ffn_b = nrm(ks[16], (DEPTH, D), 0.02)
    return {"x": x, "a_w_in": a_w_in, "a_w_out": a_w_out, "b_w_in": b_w_in, "b_f_bias": b_f_bias,
            "b_w_out": b_w_out, "peer_w_q": peer_w_q, "peer_sub_keys": peer_sub_keys,
            "peer_u": peer_u, "peer_v": peer_v, "ln_mix_g": ln_mix_g, "ln_mix_b": ln_mix_b,
            "ln_ffn_g": ln_ffn_g, "ln_ffn_b": ln_ffn_b}


def reference(x, a_w_in, a_w_out, b_w_in, b_f_bias, b_w_out, peer_w_q, peer_sub_keys,
              peer_u, peer_v, ln_mix_g, ln_mix_b, ln_ffn_g, ln_ffn_b):
    h = x
    for i in range(DEPTH):
        j = i // 2
        if i % 2 == 0:
            mix = dilated_mixer(h, a_w_in[j], a_w_out[j])
        else:
            mix = forgetting_mixer(h, b_w_in[j], b_f_bias[j], b_w_out[j])
        h = layer_norm(DN_ALPHA * h + mix, ln_mix_g[i], ln_mix_b[i])
        ffn = peer_ffn(h, peer_w_q[i], peer_sub_keys[i], peer_u[i], peer_v[i])
        h = layer_norm(DN_ALPHA * h + ffn, ln_ffn_g[i], ln_ffn_b[i])
    return h
```

```python
from contextlib import ExitStack

ENGS = ("tensor", "vector", "scalar", "gpsimd", "sync")
SEM_LIMIT = 30000
N_SLOTS = {"sync": 8, "gpsimd": 6, "scalar": 4}


class Res:
    __slots__ = ("name", "w", "r")

    def __init__(self, name=""):
        self.name = name
        self.w = None
        self.r = {}


class FW:
    def __init__(self, nc):
        self.nc = nc
        self.es = ExitStack()
        self.semn = 0
        self.eng = {}
        for n in ENGS:
            self.eng[n] = dict(sem=None, key=None, count=0, known={}, prog=[])
            self._new_sem(n)
        self.slots = {}
        for q, k in N_SLOTS.items():
            self.slots[q] = [dict(sem=self._alloc_sem(f"d{q}{i}"), uses=0) for i in range(k)]
            self.slots[q + "_i"] = 0
        self.ninstr = 0

    def _alloc_sem(self, name):
        self.semn += 1
        s = self.es.enter_context(self.nc.semaphore(f"{name}_{self.semn}"))
        return (self.semn, s)

    def _new_sem(self, n):
        e = self.eng[n]
        key, sem = self._alloc_sem(f"e{n}")
        e["sem"], e["key"], e["count"] = sem, key, 0

    def _collect(self, engname, reads, writes):
        e = self.eng[engname]
        known = e["known"]
        waits = {}

        def need(ev):
            if ev is None:
                return
            key, sem, val, src = ev
            if src == engname and engname == "tensor":
                return
            if known.get(key, 0) >= val:
                return
            if key not in waits or waits[key][1] < val:
                waits[key] = (sem, val)

        for r in reads:
            need(r.w)
        for w in writes:
            need(w.w)
            for ev in w.r.values():
                need(ev)
        for key, (sem, val) in waits.items():
            known[key] = val
        return list(waits.values())

    def _mark(self, ev, reads, writes):
        key = ev[0]
        for r in reads:
            r.r[key] = ev
        for w in writes:
            w.w = ev
            w.r = {}

    def op(self, engname, fn, reads=(), writes=()):
        e = self.eng[engname]
        if e["count"] >= SEM_LIMIT:
            self._new_sem(engname)
        waits = self._collect(engname, reads, writes)
        e["count"] += 1
        ev = (e["key"], e["sem"], e["count"], engname)
        self._mark(ev, reads, writes)
        e["prog"].append((waits, [fn], e["sem"], 1))
        self.ninstr += 1
        return ev

    def group(self, engname, fns, reads=(), writes=()):
        e = self.eng[engname]
        if e["count"] >= SEM_LIMIT:
            self._new_sem(engname)
        waits = self._collect(engname, reads, writes)
        e["count"] += 1
        ev = (e["key"], e["sem"], e["count"], engname)
        self._mark(ev, reads, writes)
        e["prog"].append((waits, list(fns), e["sem"], 1))
        self.ninstr += len(fns)
        return ev

    def dma(self, queue, out, in_, reads=(), writes=(), **kw):
        e = self.eng[queue]
        sl = self.slots[queue]
        i = self.slots[queue + "_i"]
        self.slots[queue + "_i"] = (i + 1) % len(sl)
        slot = sl[i]
        key, sem = slot["sem"]
        waits = self._collect(queue, reads, writes)
        if slot["uses"] > 0:
            prev = 16 * slot["uses"]
            if e["known"].get(key, 0) < prev:
                e["known"][key] = prev
                waits = [w for w in waits if w[0] is not sem] + [(sem, prev)]
        slot["uses"] += 1
        ev = (key, sem, 16 * slot["uses"], "dma")
        self._mark(ev, reads, writes)

        def fn(h, out=out, in_=in_, kw=kw):
            return h.dma_start(out=out, in_=in_, **kw)

        e["prog"].append((waits, [fn], sem, 16))
        self.ninstr += 1
        return ev

    def barrier(self):
        evs = []
        for n in ENGS:
            e = self.eng[n]
            if e["count"] > 0:
                evs.append((e["key"], e["sem"], e["count"], n))
        for q in N_SLOTS:
            for slot in self.slots[q]:
                if slot["uses"] > 0:
                    key, sem = slot["sem"]
                    evs.append((key, sem, 16 * slot["uses"], "dma"))
        for n in ENGS:
            e = self.eng[n]
            waits = []
            for key, sem, val, src in evs:
                if src == n and n != "sync" and n != "gpsimd" and n != "scalar":
                    continue
                if src == n:
                    continue
                if e["known"].get(key, 0) < val:
                    e["known"][key] = val
                    waits.append((sem, val))
            if waits:
                e["prog"].append((waits, [], None, 0))

    def flush(self):
        nc = self.nc
        progs = {n: self.eng[n]["prog"] for n in ENGS}

        def replay(h, prog):
            for waits, fns, sem, inc in prog:
                for s, v in waits:
                    h.wait_ge(s, v)
                ins = None
                for f in fns:
                    ins = f(h)
                if ins is not None and sem is not None:
                    ins.then_inc(sem, inc)

        with nc.Block() as block:
            @block.sync
            def _(h):
                replay(h, progs["sync"])

            @block.gpsimd
            def _(h):
                replay(h, progs["gpsimd"])

            @block.scalar
            def _(h):
                replay(h, progs["scalar"])

            @block.vector
            def _(h):
                replay(h, progs["vector"])

            @block.tensor
            def _(h):
                replay(h, progs["tensor"])

        for n in ENGS:
            self.eng[n]["prog"] = []

    def close(self):
        self.es.close()

import math
from contextlib import ExitStack
import numpy as np
import concourse.bass as bass
import concourse.mybir as mybir

F32 = mybir.dt.float32
BF16 = mybir.dt.bfloat16
AF = mybir.ActivationFunctionType
ALU = mybir.AluOpType
AX = mybir.AxisListType

S = 2048
D = 4096
NT = S // 128
KC = D // 128
ALPHA = 4.0 ** 0.25
LN_EPS = 1e-5
NEG = -1e30
QSCALE = 1.0 / math.sqrt(128.0)
NEXP = 16384
NEC = NEXP // 128


class T:
    def __init__(self, h, name=""):
        self.h = h
        self.res = Res(name)

    def __getitem__(self, k):
        return self.h[k]


class Ctx:
    def __init__(self, nc):
        self.nc = nc
        self.fw = FW(nc)
        self.n = 0

    def sb(self, es, shape, dt, name):
        self.n += 1
        return T(es.enter_context(self.nc.sbuf_tensor(f"{name}_{self.n}", list(shape), dt)), name)

    def psum(self, es):
        self.n += 1
        hf = es.enter_context(self.nc.psum_tensor(f"psf_{self.n}", [128, 6 * 512], F32))
        hb = es.enter_context(self.nc.psum_tensor(f"psb_{self.n}", [128, 2 * 1024], BF16))
        psb = [hf[:, i * 512:(i + 1) * 512] for i in range(6)] + [hb[:, i * 1024:(i + 1) * 1024] for i in range(2)]
        psres = [Res(f"ps{i}") for i in range(8)]
        return psb, psres


def evac_copy(fw, i, out, in_, reads, writes, scale=None):
    if i % 2 == 0:
        if scale is None:
            fw.op("scalar", lambda h: h.copy(out=out, in_=in_), reads, writes)
        else:
            fw.op("scalar", lambda h: h.mul(out=out, in_=in_, mul=scale), reads, writes)
    else:
        if scale is None:
            fw.op("vector", lambda h: h.tensor_copy(out=out, in_=in_), reads, writes)
        else:
            fw.op("vector", lambda h: h.tensor_scalar(out=out, in0=in_, scalar1=scale, scalar2=None, op0=ALU.mult),
                  reads, writes)


def load_consts(cx, es, c_ident):
    fw = cx.fw
    identf = cx.sb(es, [128, 128], F32, "identf")
    identb = cx.sb(es, [128, 128], BF16, "identb")
    fw.dma("sync", identf[:], c_ident, writes=[identf.res])
    fw.op("vector", lambda h: h.tensor_copy(out=identb[:], in_=identf[:]), [identf.res], [identb.res])
    return identf, identb


def build_HT(cx, Hsrc, Hres, HT, tile0, ntiles, identb, psb, psres, xs, xb, dw=D, r=1):
    fw = cx.fw
    i = 0
    for tt in range(ntiles):
        r0 = (tile0 + tt) * 128
        for d0 in range(0, D, dw):
            b = i % len(xs)
            i += 1
            fw.dma("sync", xs[b][:], Hsrc[r0:r0 + 128, d0:d0 + dw], reads=[Hres], writes=[xs[b].res])
            fw.op("vector", lambda h, b=b: h.tensor_copy(out=xb[b][:], in_=xs[b][:]), [xs[b].res], [xb[b].res])
            for grp in range(dw // 1024):
                pb = grp % 2
                pt = psb[pb]
                kc0 = d0 // 128 + grp * 8
                fns = []
                for j in range(8):
                    fns.append(lambda h, j=j, grp=grp, pt=pt, b=b: h.transpose(
                        out=pt[:, j * 128:(j + 1) * 128], in_=xb[b][:, (grp * 8 + j) * 128:(grp * 8 + j + 1) * 128],
                        identity=identb[:]))
                fw.group("tensor", fns, [xb[b].res, identb.res], [psres[pb]])
                if r == 1:
                    evac_copy(fw, grp, HT[:, kc0:kc0 + 8, tt * 128:(tt + 1) * 128],
                              pt[:, 0:1024].rearrange("p (a b) -> p a b", a=8), [psres[pb]], [HT.res])
                else:
                    w = 128 // r
                    dst = HT[:, kc0:kc0 + 8, :].rearrange("p a (rho m) -> p a rho m", rho=r)[:, :, :, tt * w:(tt + 1) * w]
                    src = pt[:, 0:1024].rearrange("p (a jm jr) -> p a jr jm", a=8, jr=r)
                    evac_copy(fw, grp, dst, src, [psres[pb]], [HT.res])


def ht_chunk(HT, kc, r, tc):
    if r == 1:
        return HT[:, kc, tc * 512:(tc + 1) * 512]
    v = HT[:, kc, :].rearrange("p (m r) -> p r m", r=r)
    if r == 4:
        return v[:, tc, :]
    return v[:, 4 * tc:4 * tc + 4, :]


def proj_stage(cx, X, Xres, w, blocks, c_ident, gate=None):
    nc, fw = cx.nc, cx.fw
    with ExitStack() as es:
        identf, identb = load_consts(cx, es, c_ident)
        psb, psres = cx.psum(es)
        HT = cx.sb(es, [128, KC, S], BF16, "HT")
        xs = [cx.sb(es, [128, 2048], F32, "xs") for _ in range(2)]
        xb = [cx.sb(es, [128, 2048], BF16, "xb") for _ in range(2)]
        cur_r = None
        Wt = [cx.sb(es, [128, KC, 128], BF16, "Wt") for _ in range(3)]
        stgb = [cx.sb(es, [128, S], BF16, "stgb") for _ in range(2)]
        stgf = [cx.sb(es, [128, S], F32, "stgf") for _ in range(2)]
        ev = 0
        for bi, (col0, width, r, dst, dst_res, scale, dt) in enumerate(blocks):
            if r != cur_r:
                build_HT(cx, X, Xres, HT, 0, NT, identb, psb[6:8], psres[6:8], xs, xb, dw=2048, r=r)
                cur_r = r
            wt = Wt[bi % 3]
            stg = (stgb if dt == BF16 else stgf)[bi % 2]
            fw.dma("gpsimd", wt[:, :, 0:width], w[:, col0:col0 + width].rearrange("(kc p) c -> p kc c", p=128),
                   writes=[wt.res])
            for tc in range(4):
                pb = tc % 4
                outp = psb[pb][0:width, :]
                fns = [lambda h, kc=kc, tc=tc, outp=outp, wt=wt, r=r, width=width: h.matmul(
                    outp, lhsT=wt[:, kc, 0:width], rhs=HT[:, kc, tc * 512:(tc + 1) * 512],
                    start=(kc == 0), stop=(kc == KC - 1)) for kc in range(KC)]
                fw.group("tensor", fns, [wt.res, HT.res], [psres[pb]])
                evac_copy(fw, ev, stg[0:width, tc * 512:(tc + 1) * 512], psb[pb][0:width, :], [psres[pb]], [stg.res],
                          scale=scale)
                ev += 1
            fw.dma("sync", dst, stg[0:width, :], reads=[stg.res], writes=[dst_res])
        if gate is not None:
            col0, fbias, negc_d, negc_res = gate
            wt = Wt[0]
            fw.dma("gpsimd", wt[:, :, 0:32], w[:, col0:col0 + 32].rearrange("(kc p) c -> p kc c", p=128),
                   writes=[wt.res])
            fb = cx.sb(es, [32, 2], F32, "fb")
            fw.dma("sync", fb[:, 0:1], fbias.rearrange("(h o) -> h o", o=1), writes=[fb.res])
            fw.op("vector", lambda h: h.tensor_scalar(out=fb[:, 1:2], in0=fb[:, 0:1], scalar1=-1.0, scalar2=None,
                                                      op0=ALU.mult), [fb.res], [fb.res])
            sp = stgf[0]
            ngc = stgf[1]
            one = cx.sb(es, [32, 1], F32, "one")
            fw.op("vector", lambda h: h.memset(one[:], 1.0), [], [one.res])
            for tc in range(4):
                pb = tc % 4
                fns = [lambda h, kc=kc, tc=tc, pb=pb: h.matmul(
                    psb[pb][0:32, :], lhsT=wt[:, kc, 0:32], rhs=HT[:, kc, tc * 512:(tc + 1) * 512],
                    start=(kc == 0), stop=(kc == KC - 1)) for kc in range(KC)]
                fw.group("tensor", fns, [wt.res, HT.res], [psres[pb]])
                fw.op("scalar", lambda h, tc=tc, pb=pb: h.activation(
                    out=sp[0:32, tc * 512:(tc + 1) * 512], in_=psb[pb][0:32, :], func=AF.Exp, bias=fb[:, 1:2], scale=-1.0),
                    [psres[pb], fb.res], [sp.res])
            fw.op("scalar", lambda h: h.activation(out=sp[0:32, :], in_=sp[0:32, :], func=AF.Ln, bias=1.0, scale=1.0),
                  [sp.res], [sp.res])
            fw.op("vector", lambda h: h.tensor_tensor_scan(
                out=ngc[0:32, :], data0=one[:, 0:1].to_broadcast([32, S]), data1=sp[0:32, :], initial=0.0,
                op0=ALU.mult, op1=ALU.add), [one.res, sp.res], [ngc.res])
            fw.dma("sync", negc_d, ngc[0:32, :], reads=[ngc.res], writes=[negc_res])
        fw.barrier()
        fw.flush()


def attn_stage(cx, PROJ, PROJ_res, mode, att, att_res, consts, negc_d=None, negc_res=None, side=None):
    nc, fw = cx.nc, cx.fw
    c_ident, c_dist, c_mask, c_caus = consts
    with ExitStack() as es:
        identf, identb = load_consts(cx, es, c_ident)
        psb, psres = cx.psum(es)
        QKV = [[cx.sb(es, [128, S], BF16, "QKV") for _ in range(3)] for _ in range(2)]
        Vt = [cx.sb(es, [128, NT, 128], BF16, "Vt") for _ in range(2)]
        if mode == "dil":
            dist = cx.sb(es, [128, 256], F32, "dist")
            mask = cx.sb(es, [128, 256], F32, "mask")
            fw.dma("sync", dist[:], c_dist, writes=[dist.res])
            fw.dma("sync", mask[:], c_mask, writes=[mask.res])
            bias = [cx.sb(es, [128, 256], F32, "bias") for _ in range(2)]
            NKMAX = 256
            heads = [(g, h) for g in range(3) for h in range(16)]
        else:
            caus = cx.sb(es, [128, 128], F32, "caus")
            fw.dma("sync", caus[:], c_caus, writes=[caus.res])
            NKMAX = S
            heads = [(0, h) for h in range(32)]
            negcb = [cx.sb(es, [128, S], F32, "negcb") for _ in range(2)]
        Sb = [cx.sb(es, [128, NKMAX], F32, "Sb") for _ in range(2)]
        Pb = [cx.sb(es, [128, NKMAX], BF16, "Pb") for _ in range(2)]
        PT = [cx.sb(es, [128, NKMAX // 128, 128], BF16, "PT") for _ in range(2)]
        OUT = [cx.sb(es, [128, 136], F32, "OUT") for _ in range(4)]

        side_it = side(es, identb, psb, psres) if side is not None else iter(())
        side_per_head = -(-NEC // len(heads))
        for idx, (g, hh) in enumerate(heads):
            b = idx % 2
            r = (1, 4, 16)[g] if mode == "dil" else 1
            QT, KT, VT = QKV[b]
            for s in range(3):
                blk = g * 48 + s * 16 + hh if mode == "dil" else s * 32 + hh
                fw.dma("sync", QKV[b][s][:], PROJ[blk], reads=[PROJ_res], writes=[QKV[b][s].res])
            for half in range(2):
                fns = [lambda h, j=j, half=half, VT=VT: h.transpose(
                    out=psb[6][:, j * 128:(j + 1) * 128], in_=VT[:, (half * 8 + j) * 128:(half * 8 + j + 1) * 128],
                    identity=identb[:]) for j in range(8)]
                fw.group("tensor", fns, [VT.res, identb.res], [psres[6]])
                evac_copy(fw, half, Vt[b][:, half * 8:half * 8 + 8, :],
                          psb[6][:, 0:1024].rearrange("p (a b) -> p a b", a=8), [psres[6]], [Vt[b].res])
            if mode == "dil":
                slope = 2.0 ** (-8.0 * (hh + 1.0) / 16.0)
                fw.op("vector", lambda h, b=b, slope=slope, r=r: h.scalar_tensor_tensor(
                    out=bias[b][:], in0=dist[:], scalar=-slope * r, in1=mask[:], op0=ALU.mult, op1=ALU.add),
                    [dist.res, mask.res], [bias[b].res])
            else:
                fw.dma("sync", negcb[b][:], negc_d[hh].partition_broadcast(128), reads=[negc_res], writes=[negcb[b].res])
            def blk_params(qi):
                if mode == "dil":
                    nb = (S // r) // 128
                    rho, blk = qi // nb, qi % nb
                    k0 = qi - 1 if blk > 0 else qi
                    nk = (qi + 1 - k0) * 128
                    boff = 0 if blk > 0 else 128
                    return rho, blk, k0, nk, boff
                return 0, 0, 0, (qi + 1) * 128, 0

            def front(qi):
                sbi = qi % 2
                rho, blk, k0, nk, boff = blk_params(qi)
                ot = OUT[qi % 4]
                nchunk = (nk + 511) // 512
                for c in range(nchunk):
                    n0 = c * 512
                    n1 = min(nk, n0 + 512)
                    bk = (qi + c) % 4
                    fw.op("tensor", lambda h, qi=qi, n0=n0, n1=n1, bk=bk, k0=k0, QT=QT, KT=KT: h.matmul(
                        psb[bk][:, 0:n1 - n0], lhsT=QT[:, qi * 128:(qi + 1) * 128],
                        rhs=KT[:, k0 * 128 + n0:k0 * 128 + n1], start=True, stop=True),
                        [QT.res, KT.res], [psres[bk]])
                    if mode == "dil":
                        fw.op("vector", lambda h, bk=bk, n1=n1, sbi=sbi, boff=boff, b=b: h.tensor_tensor(
                            out=Sb[sbi][:, 0:n1], in0=psb[bk][:, 0:n1], in1=bias[b][:, boff:boff + n1], op=ALU.add),
                            [psres[bk], bias[b].res], [Sb[sbi].res])
                    else:
                        fw.op("vector", lambda h, bk=bk, n0=n0, n1=n1, sbi=sbi, b=b: h.tensor_tensor(
                            out=Sb[sbi][:, n0:n1], in0=psb[bk][:, 0:n1 - n0], in1=negcb[b][:, n0:n1], op=ALU.add),
                            [psres[bk], negcb[b].res], [Sb[sbi].res])
                if mode == "fox":
                    fw.op("vector", lambda h, sbi=sbi, nk=nk: h.tensor_tensor(
                        out=Sb[sbi][:, nk - 128:nk], in0=Sb[sbi][:, nk - 128:nk], in1=caus[:], op=ALU.add),
                        [Sb[sbi].res, caus.res], [Sb[sbi].res])
                fw.op("vector", lambda h, sbi=sbi, nk=nk, ot=ot: h.reduce_max(out=ot[:, 128:129], in_=Sb[sbi][:, 0:nk], axis=AX.X),
                      [Sb[sbi].res], [ot.res])
                fw.op("vector", lambda h, ot=ot: h.tensor_scalar(out=ot[:, 130:131], in0=ot[:, 128:129], scalar1=-1.0,
                                                                 scalar2=None, op0=ALU.mult), [ot.res], [ot.res])
                fw.op("scalar", lambda h, sbi=sbi, nk=nk, ot=ot: h.activation(
                    out=Pb[sbi][:, 0:nk], in_=Sb[sbi][:, 0:nk], func=AF.Exp, bias=ot[:, 130:131], scale=1.0,
                    accum_out=ot[:, 129:130]), [Sb[sbi].res, ot.res], [Pb[sbi].res, ot.res])

            def back(qi):
                sbi = qi % 2
                rho, blk, k0, nk, boff = blk_params(qi)
                ot = OUT[qi % 4]
                nkt = nk // 128
                for c0 in range(0, nkt, 8):
                    c1 = min(nkt, c0 + 8)
                    pbk = 6 + (c0 // 8) % 2
                    fns = [lambda h, j=j, sbi=sbi, pbk=pbk: h.transpose(
                        out=psb[pbk][:, (j % 8) * 128:(j % 8 + 1) * 128], in_=Pb[sbi][:, j * 128:(j + 1) * 128],
                        identity=identb[:]) for j in range(c0, c1)]
                    fw.group("tensor", fns, [Pb[sbi].res, identb.res], [psres[pbk]])
                    evac_copy(fw, qi + c0 // 8, PT[sbi][:, c0:c1, :],
                              psb[pbk][:, 0:(c1 - c0) * 128].rearrange("p (a b) -> p a b", a=c1 - c0),
                              [psres[pbk]], [PT[sbi].res])
                ob = 4 + qi % 2
                fns = [lambda h, j=j, sbi=sbi, k0=k0, nkt=nkt, ob=ob, b=b: h.matmul(
                    psb[ob][:, 0:128], lhsT=PT[sbi][:, j, :], rhs=Vt[b][:, k0 + j, :], start=(j == 0), stop=(j == nkt - 1))
                    for j in range(nkt)]
                fw.group("tensor", fns, [PT[sbi].res, Vt[b].res], [psres[ob]])
                fw.op("vector", lambda h, ot=ot: h.reciprocal(out=ot[:, 131:132], in_=ot[:, 129:130]), [ot.res], [ot.res])
                fw.op("vector", lambda h, ot=ot, ob=ob: h.tensor_scalar(
                    out=ot[:, 0:128], in0=psb[ob][:, 0:128], scalar1=ot[:, 131:132], scalar2=None, op0=ALU.mult),
                    [psres[ob], ot.res], [ot.res])
                if mode == "dil":
                    dst = att[g].rearrange("(m r) h c -> r m h c", r=r)[rho, blk * 128:(blk + 1) * 128, hh, 0:130]
                    fw.dma("sync", dst, ot[:, 0:130], reads=[ot.res], writes=[att_res])
                else:
                    fw.dma("sync", att[qi * 128:(qi + 1) * 128, hh * 128:(hh + 1) * 128], ot[:, 0:128],
                           reads=[ot.res], writes=[att_res])


            front(0)
            for qi in range(NT):
                if qi % 4 == 0:
                    next(side_it, None)
                if qi + 1 < NT:
                    front(qi + 1)
                back(qi)
        for _ in side_it:
            pass
        fw.barrier()
        fw.flush()


def layer_norm_tile(cx, Y, Yres_t, gB, bB, st, mv):
    fw = cx.fw
    fns = [lambda h, c=c: h.bn_stats(out=st[:, c, :], in_=Y[:, c * 512:(c + 1) * 512]) for c in range(8)]
    fw.group("vector", fns, [Yres_t], [st.res])
    fw.op("vector", lambda h: h.bn_aggr(out=mv[:, 0:2], in_=st[:].rearrange("p a b -> p (a b)")), [st.res], [mv.res])
    fw.op("vector", lambda h: h.tensor_scalar(out=mv[:, 2:3], in0=mv[:, 1:2], scalar1=LN_EPS, scalar2=None,
                                              op0=ALU.add), [mv.res], [mv.res])
    fw.op("scalar", lambda h: h.sqrt(out=mv[:, 3:4], in_=mv[:, 2:3]), [mv.res], [mv.res])
    fw.op("vector", lambda h: h.reciprocal(out=mv[:, 2:3], in_=mv[:, 3:4]), [mv.res], [mv.res])
    fw.op("vector", lambda h: h.tensor_scalar(out=mv[:, 3:4], in0=mv[:, 0:1], scalar1=mv[:, 2:3], scalar2=-1.0,
                                              op0=ALU.mult, op1=ALU.mult), [mv.res], [mv.res])
    fw.op("scalar", lambda h: h.activation(out=Y, in_=Y, func=AF.Identity, bias=mv[:, 3:4], scale=mv[:, 2:3]),
          [Yres_t, mv.res], [Yres_t])
    fw.op("vector", lambda h: h.tensor_tensor(out=Y, in0=Y, in1=gB[:], op=ALU.mult), [Yres_t, gB.res], [Yres_t])
    fw.op("vector", lambda h: h.tensor_tensor(out=Y, in0=Y, in1=bB[:], op=ALU.add), [Yres_t, bB.res], [Yres_t])


def ln_stage(cx, Ypre, Ypre_res, ln_g, ln_b, Hout, Hout_res):
    nc, fw = cx.nc, cx.fw
    with ExitStack() as es:
        gB = cx.sb(es, [128, D], F32, "gB")
        bB = cx.sb(es, [128, D], F32, "bB")
        fw.dma("sync", gB[:], ln_g.partition_broadcast(128), writes=[gB.res])
        fw.dma("sync", bB[:], ln_b.partition_broadcast(128), writes=[bB.res])
        Y = [cx.sb(es, [128, D], F32, "Y") for _ in range(3)]
        st = cx.sb(es, [128, 8, 6], F32, "st")
        mv = cx.sb(es, [128, 4], F32, "mv")
        for tile in range(NT):
            y = Y[tile % 3]
            fw.dma("sync", y[:], Ypre[tile * 128:(tile + 1) * 128, :], reads=[Ypre_res], writes=[y.res])
            layer_norm_tile(cx, y[:], y.res, gB, bB, st, mv)
            fw.dma("sync", Hout[tile * 128:(tile + 1) * 128, :], y[:], reads=[y.res], writes=[Hout_res])
        fw.barrier()
        fw.flush()


def mixout_stage(cx, X, Xres, att, att_res, w_out, KO, Ypre, Ypre_res, mode, c_ident):
    nc, fw = cx.nc, cx.fw
    KOC = KO // 128
    TG = 4
    with ExitStack() as es:
        identf, identb = load_consts(cx, es, c_ident)
        psb, psres = cx.psum(es)
        if mode == "dil":
            A = [cx.sb(es, [128, 16, 132], F32, "A") for _ in range(3)]
            wg = cx.sb(es, [128, 4, 16], F32, "wg")
            tmp = cx.sb(es, [128, 16, 128], F32, "tmp")
        else:
            A = [cx.sb(es, [128, KO], F32, "A") for _ in range(2)]
        mg = [cx.sb(es, [128, KO], BF16, "mg") for _ in range(2)]
        MT = cx.sb(es, [128, KOC, TG * 128], BF16, "MT")
        Wo = [cx.sb(es, [128, KOC, 512], BF16, "Wo") for _ in range(2)]
        Xc = [cx.sb(es, [128, TG, 512], F32, "Xc") for _ in range(2)]
        wcount = 0
        for tg in range(NT // TG):
            for tt in range(TG):
                tile = tg * TG + tt
                r0 = tile * 128
                if mode == "dil":
                    for g in range(3):
                        fw.dma("sync", A[g][:], att[g, r0:r0 + 128, :, :], reads=[att_res], writes=[A[g].res])
                    ares = [A[g].res for g in range(3)]
                    for g in range(3):
                        fw.op("scalar", lambda h, g=g: h.activation(out=A[g][:, :, 129], in_=A[g][:, :, 129], func=AF.Ln),
                              [A[g].res], [A[g].res])
                        fw.op("vector", lambda h, g=g: h.tensor_tensor(out=A[g][:, :, 128], in0=A[g][:, :, 128],
                                                                       in1=A[g][:, :, 129], op=ALU.add),
                              [A[g].res], [A[g].res])
                    lse = [A[g][:, :, 128] for g in range(3)]
                    fw.op("vector", lambda h, lse=lse: h.tensor_tensor(out=wg[:, 3, :], in0=lse[0], in1=lse[1], op=ALU.max),
                          ares, [wg.res])
                    fw.op("vector", lambda h, lse=lse: h.tensor_tensor(out=wg[:, 3, :], in0=wg[:, 3, :], in1=lse[2], op=ALU.max),
                          ares + [wg.res], [wg.res])
                    for g in range(3):
                        fw.op("vector", lambda h, g=g, lse=lse: h.tensor_tensor(out=wg[:, g, :], in0=lse[g], in1=wg[:, 3, :],
                                                                                op=ALU.subtract), ares + [wg.res], [wg.res])
                    fw.op("scalar", lambda h: h.activation(out=wg[:, 0:3, :], in_=wg[:, 0:3, :], func=AF.Exp),
                          [wg.res], [wg.res])
                    fw.op("vector", lambda h: h.tensor_tensor(out=wg[:, 3, :], in0=wg[:, 0, :], in1=wg[:, 1, :], op=ALU.add),
                          [wg.res], [wg.res])
                    fw.op("vector", lambda h: h.tensor_tensor(out=wg[:, 3, :], in0=wg[:, 3, :], in1=wg[:, 2, :], op=ALU.add),
                          [wg.res], [wg.res])
                    fw.op("vector", lambda h: h.reciprocal(out=wg[:, 3, :], in_=wg[:, 3, :]), [wg.res], [wg.res])
                    for g in range(3):
                        fw.op("vector", lambda h, g=g: h.tensor_tensor(out=wg[:, g, :], in0=wg[:, g, :], in1=wg[:, 3, :],
                                                                       op=ALU.mult), [wg.res], [wg.res])
                    m = mg[tile % 2]
                    mv3 = m[:].rearrange("p (h c) -> p h c", c=128)
                    fw.op("vector", lambda h: h.tensor_tensor(out=tmp[:], in0=A[0][:, :, 0:128],
                                                              in1=wg[:, 0, :].unsqueeze(2).to_broadcast([128, 16, 128]),
                                                              op=ALU.mult), [A[0].res, wg.res], [tmp.res])
                    for g in (1, 2):
                        fw.op("vector", lambda h, g=g: h.tensor_tensor(
                            out=A[g][:, :, 0:128], in0=A[g][:, :, 0:128],
                            in1=wg[:, g, :].unsqueeze(2).to_broadcast([128, 16, 128]), op=ALU.mult),
                            [A[g].res, wg.res], [A[g].res])
                    fw.op("vector", lambda h: h.tensor_tensor(out=tmp[:], in0=tmp[:], in1=A[1][:, :, 0:128], op=ALU.add),
                          [tmp.res, A[1].res], [tmp.res])
                    fw.op("vector", lambda h, mv3=mv3: h.tensor_tensor(out=mv3, in0=tmp[:], in1=A[2][:, :, 0:128], op=ALU.add),
                          [tmp.res, A[2].res], [m.res])
                else:
                    a = A[tile % 2]
                    m = mg[tile % 2]
                    fw.dma("sync", a[:], att[r0:r0 + 128, :], reads=[att_res], writes=[a.res])
                    fw.op("scalar", lambda h, a=a, m=m: h.copy(out=m[:], in_=a[:]), [a.res], [m.res])
                for c0 in range(0, KOC, 8):
                    pbk = 6 + (c0 // 8) % 2
                    fns = [lambda h, j=j, m=m, pbk=pbk: h.transpose(
                        out=psb[pbk][:, (j % 8) * 128:(j % 8 + 1) * 128], in_=m[:, j * 128:(j + 1) * 128],
                        identity=identb[:]) for j in range(c0, c0 + 8)]
                    fw.group("tensor", fns, [m.res, identb.res], [psres[pbk]])
                    evac_copy(fw, c0 // 8, MT[:, c0:c0 + 8, tt * 128:(tt + 1) * 128],
                              psb[pbk][:, 0:1024].rearrange("p (a b) -> p a b", a=8), [psres[pbk]], [MT.res])
            for dc in range(D // 512):
                wb = wcount % 2
                wcount += 1
                fw.dma("gpsimd", Wo[wb][:], w_out[:, dc * 512:(dc + 1) * 512].rearrange("(kc p) c -> p kc c", p=128),
                       writes=[Wo[wb].res])
                fw.dma("sync", Xc[wb][:], X[tg * TG * 128:(tg + 1) * TG * 128, dc * 512:(dc + 1) * 512]
                       .rearrange("(t p) c -> p t c", p=128), reads=[Xres], writes=[Xc[wb].res])
                for tt in range(TG):
                    pb = (dc * TG + tt) % 6
                    fns = [lambda h, kc=kc, tt=tt, pb=pb, wb=wb: h.matmul(
                        psb[pb], lhsT=MT[:, kc, tt * 128:(tt + 1) * 128], rhs=Wo[wb][:, kc, :],
                        start=(kc == 0), stop=(kc == KOC - 1)) for kc in range(KOC)]
                    fw.group("tensor", fns, [MT.res, Wo[wb].res], [psres[pb]])
                    fw.op("vector", lambda h, tt=tt, pb=pb, wb=wb: h.scalar_tensor_tensor(
                        out=Xc[wb][:, tt, :], in0=Xc[wb][:, tt, :], scalar=ALPHA, in1=psb[pb],
                        op0=ALU.mult, op1=ALU.add), [Xc[wb].res, psres[pb]], [Xc[wb].res])
                fw.dma("sync", Ypre[tg * TG * 128:(tg + 1) * TG * 128, dc * 512:(dc + 1) * 512]
                       .rearrange("(t p) c -> p t c", p=128), Xc[wb][:], reads=[Xc[wb].res], writes=[Ypre_res])
        fw.barrier()
        fw.flush()


def prep_gen(cx, es, u, v, UTs, UTs_res, Vs, Vs_res, identb, psb, psres):
    fw = cx.fw
    NBUF = 3
    Ub = [cx.sb(es, [128, D], BF16, "Ub") for _ in range(NBUF)]
    UT = [cx.sb(es, [128, KC, 128], BF16, "UTp") for _ in range(2)]
    Vb = [cx.sb(es, [128, D], BF16, "Vbp") for _ in range(NBUF)]

    def load(E):
        fw.dma("gpsimd", Ub[E % NBUF][:], u[E * 128:(E + 1) * 128, :], writes=[Ub[E % NBUF].res], max_dma_last_dim=8192)
        fw.dma("gpsimd", Vb[E % NBUF][:], v[E * 128:(E + 1) * 128, :], writes=[Vb[E % NBUF].res], max_dma_last_dim=8192)

    def compute(E):
        ub, ut, vb = Ub[E % NBUF], UT[E % 2], Vb[E % NBUF]
        for grp in range(KC // 8):
            pbk = 6 + grp % 2
            fns = [lambda h, j=j, grp=grp, ub=ub, pbk=pbk: h.transpose(
                out=psb[pbk][:, j * 128:(j + 1) * 128], in_=ub[:, (grp * 8 + j) * 128:(grp * 8 + j + 1) * 128],
                identity=identb[:]) for j in range(8)]
            fw.group("tensor", fns, [ub.res, identb.res], [psres[pbk]])
            evac_copy(fw, grp, ut[:, grp * 8:grp * 8 + 8, :],
                      psb[pbk][:, 0:1024].rearrange("p (a b) -> p a b", a=8), [psres[pbk]], [ut.res])
        fw.dma("sync", UTs[E], ut[:], reads=[ut.res], writes=[UTs_res])
        fw.dma("sync", Vs[:, :, E, :].rearrange("c e d -> e c d"), vb[:].rearrange("e (c d) -> e c d", c=8),
               reads=[vb.res], writes=[Vs_res])

    load(0)
    load(1)
    for E in range(NEC):
        compute(E)
        if E + 2 < NEC:
            load(E + 2)
        yield E


def prep_experts(cx, u, v, UTs, UTs_res, Vs, Vs_res, c_ident):
    nc, fw = cx.nc, cx.fw
    with ExitStack() as es:
        identf, identb = load_consts(cx, es, c_ident)
        psb, psres = cx.psum(es)
        for _ in prep_gen(cx, es, u, v, UTs, UTs_res, Vs, Vs_res, identb, psb, psres):
            pass
        fw.barrier()
        fw.flush()


def peer_stage(cx, Hin, Hin_res, PQ, PQ_res, keys, UTs, UTs_res, Vs, Vs_res, Ypre, Ypre_res, c_ident):
    nc, fw = cx.nc, cx.fw
    TG = 2
    TW = TG * 128
    NIB = NEC // 4
    with ExitStack() as es:
        identf, identb = load_consts(cx, es, c_ident)
        psb, psres = cx.psum(es)
        kraw = cx.sb(es, [128, 2, 128], F32, "kraw")
        kT = cx.sb(es, [128, 2, 128], F32, "kT")
        for half in range(2):
            fw.dma("sync", kraw[:, half, :], keys[half], writes=[kraw.res])
        for half in range(2):
            fw.op("tensor", lambda h, half=half: h.transpose(out=psb[half][:, 0:128], in_=kraw[:, half, :], identity=identf[:]),
                  [kraw.res, identf.res], [psres[half]])
            fw.op("vector", lambda h, half=half: h.tensor_copy(out=kT[:, half, :], in_=psb[half][:, 0:128]),
                  [psres[half]], [kT.res])
        xs = [cx.sb(es, [128, 2048], F32, "xs")]
        xb = [cx.sb(es, [128, 2048], BF16, "xb")]
        HTg = cx.sb(es, [128, KC, TW], BF16, "HTg")
        QTt = cx.sb(es, [128, 16, 128], F32, "QTt")
        Ssb = [cx.sb(es, [128, 16, 128], F32, "Ssb") for _ in range(TG)]
        thr = cx.sb(es, [128, TG, 8], F32, "thr")
        bg = cx.sb(es, [128, TG, 8], F32, "bg")
        T16 = cx.sb(es, [128, 2, 16], F32, "T16")
        wk = cx.sb(es, [128, 128], F32, "wk")
        cand = cx.sb(es, [128, 256], F32, "cand")
        cand2 = cx.sb(es, [128, 256], F32, "cand2")
        C16 = cx.sb(es, [128, 16], F32, "C16")
        E16 = cx.sb(es, [128, 16], F32, "E16")
        sc = cx.sb(es, [128, 4], F32, "sc")
        NSB, NGB = 3, 8
        sumt = [cx.sb(es, [128, 4, 128], F32, "sumt") for _ in range(NSB)]
        et = [cx.sb(es, [128, 4, 128], BF16, "et") for _ in range(NSB)]
        gt = [cx.sb(es, [128, 4, 128], BF16, "gt") for _ in range(NGB)]
        UT = [cx.sb(es, [128, KC, 128], BF16, "UT") for _ in range(3)]
        Vb = [cx.sb(es, [128, 8, 512], BF16, "Vb") for _ in range(4)]
        act = [cx.sb(es, [128, TW], F32, "act") for _ in range(2)]
        ACTT = cx.sb(es, [128, NEC, TW], BF16, "ACTT")
        xc = [cx.sb(es, [128, 512], F32, "xc") for _ in range(2)]
        cnt = dict(ut=0, vb=0, g=0, xc=0)

        NG = NT // TG

        def prologue(tg):
            t0 = tg * TW
            build_HT(cx, Hin, Hin_res, HTg, tg * TG, TG, identb, psb[6:8], psres[6:8], xs, xb, dw=2048)
            for tt in range(TG):
                fw.dma("sync", QTt[:], PQ[:, :, t0 + tt * 128:t0 + (tt + 1) * 128].rearrange("a p t -> p a t"),
                       reads=[PQ_res], writes=[QTt.res])
                for q4 in range(4):
                    fns = [lambda h, q4=q4, j=j: h.matmul(
                        psb[q4][:, j * 128:(j + 1) * 128], lhsT=QTt[:, q4 * 4 + j, :], rhs=kT[:, (q4 * 4 + j) % 2, :],
                        start=True, stop=True) for j in range(4)]
                    fw.group("tensor", fns, [QTt.res, kT.res], [psres[q4]])
                    evac_copy(fw, q4, Ssb[tt][:, q4 * 4:q4 * 4 + 4, :], psb[q4].rearrange("p (a b) -> p a b", a=4),
                              [psres[q4]], [Ssb[tt].res])
                for hd in range(8):
                    for half in range(2):
                        src = Ssb[tt][:, 2 * hd + half, :]
                        fw.op("vector", lambda h, src=src, half=half: h.max(out=T16[:, half, 0:8], in_=src),
                              [Ssb[tt].res], [T16.res])
                        fw.op("vector", lambda h, src=src, half=half: h.match_replace(
                            out=wk[:], in_to_replace=T16[:, half, 0:8], in_values=src, imm_value=NEG),
                            [Ssb[tt].res, T16.res], [wk.res])
                        fw.op("vector", lambda h, half=half: h.max(out=T16[:, half, 8:16], in_=wk[:]),
                              [wk.res], [T16.res])
                    fw.op("vector", lambda h: h.tensor_tensor(
                        out=cand[:].rearrange("p (a b) -> p a b", a=16),
                        in0=T16[:, 0, :].unsqueeze(2).to_broadcast([128, 16, 16]),
                        in1=T16[:, 1, :].unsqueeze(1).to_broadcast([128, 16, 16]), op=ALU.add),
                        [T16.res], [cand.res])
                    fw.op("vector", lambda h: h.max(out=C16[:, 0:8], in_=cand[:]), [cand.res], [C16.res])
                    fw.op("vector", lambda h: h.match_replace(out=cand2[:], in_to_replace=C16[:, 0:8], in_values=cand[:],
                                                              imm_value=NEG), [cand.res, C16.res], [cand2.res])
                    fw.op("vector", lambda h: h.max(out=C16[:, 8:16], in_=cand2[:]), [cand2.res], [C16.res])
                    fw.op("vector", lambda h: h.tensor_scalar(out=sc[:, 0:1], in0=C16[:, 0:1], scalar1=-1.0, scalar2=None,
                                                              op0=ALU.mult), [C16.res], [sc.res])
                    fw.op("scalar", lambda h: h.activation(out=E16[:], in_=C16[:], func=AF.Exp, bias=sc[:, 0:1], scale=1.0,
                                                           accum_out=sc[:, 1:2]), [C16.res, sc.res], [E16.res, sc.res])
                    fw.op("scalar", lambda h: h.activation(out=sc[:, 2:3], in_=sc[:, 1:2], func=AF.Ln), [sc.res], [sc.res])
                    fw.op("vector", lambda h, tt=tt, hd=hd: h.tensor_tensor(out=bg[:, tt, hd:hd + 1], in0=sc[:, 0:1],
                                                                           in1=sc[:, 2:3], op=ALU.subtract),
                          [sc.res], [bg.res])
                    fw.op("vector", lambda h, tt=tt, hd=hd: h.tensor_copy(out=thr[:, tt, hd:hd + 1], in_=C16[:, 15:16]),
                          [C16.res], [thr.res])


        def phase1(tg):
            gq = {}

            def g_add(ib, k, eng):
                tt, hd = k // 8, k % 8
                sbi = cnt["g"] % NSB
                gb = cnt["g"] % NGB
                cnt["g"] += 1
                gq[(ib, k)] = (sbi, gb)
                s0 = Ssb[tt][:, 2 * hd, ib * 4:(ib + 1) * 4]
                s1 = Ssb[tt][:, 2 * hd + 1, :]
                fw.op(eng, lambda h: h.tensor_tensor(
                    out=sumt[sbi][:], in0=s0.unsqueeze(2).to_broadcast([128, 4, 128]),
                    in1=s1.unsqueeze(1).to_broadcast([128, 4, 128]), op=ALU.add),
                    [Ssb[tt].res], [sumt[sbi].res])

            def g_rest(ib, k):
                tt, hd = k // 8, k % 8
                sbi, gb = gq[(ib, k)]
                gq[(ib, k)] = gb
                fw.op("scalar", lambda h: h.activation(
                    out=et[sbi][:], in_=sumt[sbi][:], func=AF.Exp, bias=bg[:, tt, hd:hd + 1], scale=1.0),
                    [sumt[sbi].res, bg.res], [et[sbi].res])
                fw.op("vector", lambda h: h.scalar_tensor_tensor(
                    out=gt[gb][:], in0=sumt[sbi][:], scalar=thr[:, tt, hd:hd + 1], in1=et[sbi][:],
                    op0=ALU.is_ge, op1=ALU.mult), [sumt[sbi].res, thr.res, et[sbi].res], [gt[gb].res])

            def g_batch(ib, ks):
                ks = list(ks)
                for idx in range(min(NSB, len(ks))):
                    g_add(ib, ks[idx], "vector" if idx < 2 else "gpsimd")
                for idx, k in enumerate(ks):
                    g_rest(ib, k)
                    if idx + NSB < len(ks):
                        g_add(ib, ks[idx + NSB], "gpsimd")

            def g_back(ib, k):
                par = ib % 2
                tt, hd = k // 8, k % 8
                gb = gq.pop((ib, k))
                fns = [lambda h, ec=ec: h.matmul(
                    psb[2 * par + ec // 2][:, (ec % 2) * TW + tt * 128:(ec % 2) * TW + (tt + 1) * 128],
                    lhsT=gt[gb][:, ec, :], rhs=identb[:], start=(k == 0 and ec % 2 == 0),
                    stop=(k == 8 * TG - 1 and ec % 2 == 1), skip_group_check=True) for ec in range(4)]
                fw.group("tensor", fns, [gt[gb].res, identb.res], [psres[2 * par], psres[2 * par + 1]])

            def st_front(E):
                ut = UT[cnt["ut"] % 3]
                cnt["ut"] += 1
                fw.dma("sync", ut[:], UTs[E], reads=[UTs_res], writes=[ut.res])
                sbk = 4 + E % 2
                fns = [lambda h, kc=kc: h.matmul(
                    psb[sbk][:, 0:TW], lhsT=ut[:, kc, :], rhs=HTg[:, kc, :], start=(kc == 0), stop=(kc == KC - 1))
                    for kc in range(KC)]
                fw.group("tensor", fns, [ut.res, HTg.res], [psres[sbk]])

            def st_gelu(E):
                sbk = 4 + E % 2
                a = act[E % 2]
                fw.op("scalar", lambda h: h.activation(out=a[:], in_=psb[sbk][:, 0:TW], func=AF.Gelu),
                      [psres[sbk]], [a.res])

            def st_mult(E):
                ib, ec = E // 4, E % 4
                a = act[E % 2]
                gbank = 2 * (ib % 2) + ec // 2
                fw.op("vector", lambda h: h.tensor_tensor(
                    out=ACTT[:, E, :], in0=a[:], in1=psb[gbank][:, (ec % 2) * TW:(ec % 2 + 1) * TW],
                    op=ALU.mult), [a.res, psres[gbank]], [ACTT.res])

            NK = 8 * TG
            for k0 in range(0, NK, NGB):
                g_batch(0, range(k0, k0 + NGB))
                for k in range(k0, k0 + NGB):
                    g_back(0, k)
            for ib in range(NIB):
                for pr in range(2):
                    ks = range(pr * (NK // 2), (pr + 1) * (NK // 2))
                    nxt = ib + 1 < NIB
                    if nxt:
                        g_batch(ib + 1, ks)
                    E0 = ib * 4 + pr * 2
                    st_front(E0)
                    if nxt:
                        for k in list(ks)[:len(ks) // 2]:
                            g_back(ib + 1, k)
                    st_front(E0 + 1)
                    if nxt:
                        for k in list(ks)[len(ks) // 2:]:
                            g_back(ib + 1, k)
                    st_gelu(E0)
                    st_gelu(E0 + 1)
                    st_mult(E0)
                    st_mult(E0 + 1)


        def phase2(tg):
            for dc in range(8):
                if dc == 4 and tg + 1 < NG:
                    prologue(tg + 1)
                banks = [(2 * dc) % 6, (2 * dc + 1) % 6]
                for eb in range(NEC // 8):
                    vb = Vb[cnt["vb"] % 4]
                    cnt["vb"] += 1
                    fw.dma("sync", vb[:], Vs[dc, :, eb * 8:(eb + 1) * 8, :], reads=[Vs_res], writes=[vb.res])
                    fns = []
                    for c in range(8):
                        E = eb * 8 + c
                        for tt in range(TG):
                            fns.append(lambda h, E=E, c=c, tt=tt, vb=vb: h.matmul(
                                psb[banks[tt]], lhsT=ACTT[:, E, tt * 128:(tt + 1) * 128], rhs=vb[:, c, :],
                                start=(E == 0), stop=(E == NEC - 1)))
                    fw.group("tensor", fns, [ACTT.res, vb.res], [psres[banks[0]], psres[banks[1]]])
                for tt in range(TG):
                    tile = tg * TG + tt
                    x_ = xc[cnt["xc"] % 2]
                    cnt["xc"] += 1
                    fw.dma("sync", x_[:], Hin[tile * 128:(tile + 1) * 128, dc * 512:(dc + 1) * 512],
                           reads=[Hin_res], writes=[x_.res])
                    fw.op("vector", lambda h, x_=x_, tt=tt: h.scalar_tensor_tensor(
                        out=x_[:], in0=x_[:], scalar=ALPHA, in1=psb[banks[tt]], op0=ALU.mult, op1=ALU.add),
                        [x_.res, psres[banks[tt]]], [x_.res])
                    fw.dma("sync", Ypre[tile * 128:(tile + 1) * 128, dc * 512:(dc + 1) * 512], x_[:],
                           reads=[x_.res], writes=[Ypre_res])


        prologue(0)
        for tg in range(NG):
            phase1(tg)
            phase2(tg)
        fw.barrier()
        fw.flush()

import numpy as np
import concourse.bass as bass
import concourse.mybir as mybir
from concourse.bass_utils import run_bass_kernel_spmd

ALL_STAGES = ("proj0", "attn0", "mix0", "lnm0", "pq0", "prep0", "peer0", "lnf0",
              "proj1", "attn1", "mix1", "lnm1", "pq1", "prep1", "peer1", "lnf1")


def make_consts():
    ident = np.eye(128, dtype=np.float32)
    i = np.arange(128)[:, None]
    j = np.arange(256)[None, :]
    dist = (128 + i - j).astype(np.float32)
    mask = np.where((dist >= 0) & (dist <= 128), 0.0, NEG).astype(np.float32)
    jj = np.arange(128)[None, :]
    caus = np.where(jj <= i, 0.0, NEG).astype(np.float32)
    return {"c_ident": ident, "c_dist": dist, "c_mask": mask, "c_caus": caus}


def build(stages=ALL_STAGES, dbg_out=()):
    nc = bass.Bass("TRN2", target_bir_lowering=False)
    cx = Ctx(nc)

    def din(name, shape, dt=F32):
        return nc.dram_tensor(name, list(shape), dt, kind="ExternalInput").ap()

    def dscr(name, shape, dt=F32):
        kind = "ExternalOutput" if name in dbg_out else "Internal"
        return nc.dram_tensor(name, list(shape), dt, kind=kind).ap()

    SHAPES = {"x": [S, D], "a_w_in": [D, 18432], "a_w_out": [2048, D], "b_w_in": [D, 12320], "b_f_bias": [32],
              "b_w_out": [D, D], "peer_w_q": [2, D, 2048], "peer_sub_keys": [2, 2, 128, 128],
              "peer_u": [2, NEXP, D], "peer_v": [2, NEXP, D], "ln_mix_g": [2, D], "ln_mix_b": [2, D],
              "ln_ffn_g": [2, D], "ln_ffn_b": [2, D], "c_ident": [128, 128], "c_dist": [128, 256],
              "c_mask": [128, 256], "c_caus": [128, 128]}
    cache = {}

    def I(name):
        if name not in cache:
            cache[name] = din(name, SHAPES[name])
        return cache[name]
    c_ident, c_dist, c_mask, c_caus = I("c_ident"), I("c_dist"), I("c_mask"), I("c_caus")
    consts = (c_ident, c_dist, c_mask, c_caus)

    PROJ = dscr("PROJ", [144, 128, S], BF16)
    ATT0 = dscr("ATT0", [3, S, 16, 132])
    ATT1 = dscr("ATT1", [S, D])
    YPRE = dscr("YPRE", [S, D])
    H1 = dscr("H1", [S, D])
    H2 = dscr("H2", [S, D])
    H3 = dscr("H3", [S, D])
    PQ = dscr("PQ", [16, 128, S])
    NEGC = dscr("NEGC", [32, S])
    UTs = dscr("UTs", [NEC, 128, KC, 128], BF16)
    Vs = dscr("Vs", [8, 128, NEC, 512], BF16)
    y = nc.dram_tensor("y", [S, D], F32, kind="ExternalOutput").ap()

    R = lambda: Res("d")

    def layer(i, hin, hmid, hout):
        if f"proj{i}" in stages:
            if i == 0:
                blocks = []
                for g in range(3):
                    for s in range(3):
                        for h in range(16):
                            blocks.append((g * 6144 + s * 2048 + h * 128, 128, (1, 4, 16)[g],
                                           PROJ[g * 48 + s * 16 + h], R(), QSCALE if s == 0 else None, BF16))
                proj_stage(cx, hin, R(), I("a_w_in"), blocks, c_ident)
            else:
                blocks = []
                for s in range(3):
                    for h in range(32):
                        blocks.append((s * D + h * 128, 128, 1, PROJ[s * 32 + h], R(), QSCALE if s == 0 else None, BF16))
                proj_stage(cx, hin, R(), I("b_w_in"), blocks, c_ident, gate=(3 * D, I("b_f_bias"), NEGC, R()))
        if f"attn{i}" in stages:
            side = None
            if f"prep{i}" in stages:
                ur, vr = R(), R()
                side = lambda es, identb, psb, psres: prep_gen(cx, es, I("peer_u")[i], I("peer_v")[i], UTs, ur, Vs, vr,
                                                               identb, psb, psres)
            if i == 0:
                attn_stage(cx, PROJ, R(), "dil", ATT0, R(), consts, side=side)
            else:
                attn_stage(cx, PROJ, R(), "fox", ATT1, R(), consts, NEGC, R(), side=side)
        if f"mix{i}" in stages:
            if i == 0:
                mixout_stage(cx, hin, R(), ATT0, R(), I("a_w_out"), 2048, YPRE, R(), "dil", c_ident)
            else:
                mixout_stage(cx, hin, R(), ATT1, R(), I("b_w_out"), 4096, YPRE, R(), "fox", c_ident)
        if f"lnm{i}" in stages:
            ln_stage(cx, YPRE, R(), I("ln_mix_g")[i], I("ln_mix_b")[i], hmid, R())
        if f"pq{i}" in stages:
            blocks = [(hp * 128, 128, 1, PQ[hp], R(), None, F32) for hp in range(16)]
            proj_stage(cx, hmid, R(), I("peer_w_q")[i], blocks, c_ident)
        if f"prep{i}" in stages and f"attn{i}" not in stages:
            prep_experts(cx, I("peer_u")[i], I("peer_v")[i], UTs, R(), Vs, R(), c_ident)
        if f"peer{i}" in stages:
            peer_stage(cx, hmid, R(), PQ, R(), I("peer_sub_keys")[i], UTs, R(), Vs, R(), YPRE, R(), c_ident)
        if f"lnf{i}" in stages:
            ln_stage(cx, YPRE, R(), I("ln_ffn_g")[i], I("ln_ffn_b")[i], hout, R())

    layer(0, I("x"), H1, H2)
    layer(1, H2, H3, y)
    print("instructions:", cx.fw.ninstr, flush=True)
    nc._used_inputs = list(cache)
    return nc


WEIGHTS = ("a_w_in", "a_w_out", "b_w_in", "b_f_bias", "b_w_out", "peer_w_q", "peer_sub_keys", "peer_u", "peer_v",
           "ln_mix_g", "ln_mix_b", "ln_ffn_g", "ln_ffn_b")


def make_in_maps(inputs, n):
    consts = make_consts()
    shared = {}
    for k in WEIGHTS:
        a = np.ascontiguousarray(inputs[k])
        if k in ("a_w_in", "a_w_out", "b_w_in", "b_f_bias", "b_w_out"):
            a = a[0]
        shared[k] = a
    shared.update(consts)
    maps = []
    for c in range(n):
        m = dict(shared)
        m["x"] = np.ascontiguousarray(inputs["x"][c])
        maps.append(m)
    return maps


def kernel(**inputs):
    n = 8
    nc = build()
    in_maps = make_in_maps(inputs, n)
    res = run_bass_kernel_spmd(nc, in_maps, core_ids=list(range(n)))
    return np.stack([np.asarray(r["y"]) for r in res.results], axis=0).astype(np.float32)
```

```python
from contextlib import ExitStack

ENGS = ("tensor", "vector", "scalar", "gpsimd", "sync")
SEM_LIMIT = 30000
N_SLOTS = {"sync": 8, "gpsimd": 6, "scalar": 4}


class Res:
    __slots__ = ("name", "w", "r")

    def __init__(self, name=""):
        self.name = name
        self.w = None
        self.r = {}


class FW:
    def __init__(self, nc):
        self.nc = nc
        self.es = ExitStack()
        self.semn = 0
        self.eng = {}
        for n in ENGS:
            self.eng[n] = dict(sem=None, key=None, count=0, known={}, prog=[])
            self._new_sem(n)
        self.slots = {}
        for q, k in N_SLOTS.items():
            self.slots[q] = [dict(sem=self._alloc_sem(f"d{q}{i}"), uses=0) for i in range(k)]
            self.slots[q + "_i"] = 0
        self.ninstr = 0

    def _alloc_sem(self, name):
        self.semn += 1
        s = self.es.enter_context(self.nc.semaphore(f"{name}_{self.semn}"))
        return (self.semn, s)

    def _new_sem(self, n):
        e = self.eng[n]
        key, sem = self._alloc_sem(f"e{n}")
        e["sem"], e["key"], e["count"] = sem, key, 0

    def _collect(self, engname, reads, writes):
        e = self.eng[engname]
        known = e["known"]
        waits = {}

        def need(ev):
            if ev is None:
                return
            key, sem, val, src = ev
            if src == engname and engname == "tensor":
                return
            if known.get(key, 0) >= val:
                return
            if key not in waits or waits[key][1] < val:
                waits[key] = (sem, val)

        for r in reads:
            need(r.w)
        for w in writes:
            need(w.w)
            for ev in w.r.values():
                need(ev)
        for key, (sem, val) in waits.items():
            known[key] = val
        return list(waits.values())

    def _mark(self, ev, reads, writes):
        key = ev[0]
        for r in reads:
            r.r[key] = ev
        for w in writes:
            w.w = ev
            w.r = {}

    def op(self, engname, fn, reads=(), writes=()):
        e = self.eng[engname]
        if e["count"] >= SEM_LIMIT:
            self._new_sem(engname)
        waits = self._collect(engname, reads, writes)
        e["count"] += 1
        ev = (e["key"], e["sem"], e["count"], engname)
        self._mark(ev, reads, writes)
        e["prog"].append((waits, [fn], e["sem"], 1))
        self.ninstr += 1
        return ev

    def group(self, engname, fns, reads=(), writes=()):
        e = self.eng[engname]
        if e["count"] >= SEM_LIMIT:
            self._new_sem(engname)
        waits = self._collect(engname, reads, writes)
        e["count"] += 1
        ev = (e["key"], e["sem"], e["count"], engname)
        self._mark(ev, reads, writes)
        e["prog"].append((waits, list(fns), e["sem"], 1))
        self.ninstr += len(fns)
        return ev

    def dma(self, queue, out, in_, reads=(), writes=(), **kw):
        e = self.eng[queue]
        sl = self.slots[queue]
        i = self.slots[queue + "_i"]
        self.slots[queue + "_i"] = (i + 1) % len(sl)
        slot = sl[i]
        key, sem = slot["sem"]
        waits = self._collect(queue, reads, writes)
        if slot["uses"] > 0:
            prev = 16 * slot["uses"]
            if e["known"].get(key, 0) < prev:
                e["known"][key] = prev
                waits = [w for w in waits if w[0] is not sem] + [(sem, prev)]
        slot["uses"] += 1
        ev = (key, sem, 16 * slot["uses"], "dma")
        self._mark(ev, reads, writes)

        def fn(h, out=out, in_=in_, kw=kw):
            return h.dma_start(out=out, in_=in_, **kw)

        e["prog"].append((waits, [fn], sem, 16))
        self.ninstr += 1
        return ev

    def barrier(self):
        evs = []
        for n in ENGS:
            e = self.eng[n]
            if e["count"] > 0:
                evs.append((e["key"], e["sem"], e["count"], n))
        for q in N_SLOTS:
            for slot in self.slots[q]:
                if slot["uses"] > 0:
                    key, sem = slot["sem"]
                    evs.append((key, sem, 16 * slot["uses"], "dma"))
        for n in ENGS:
            e = self.eng[n]
            waits = []
            for key, sem, val, src in evs:
                if src == n and n != "sync" and n != "gpsimd" and n != "scalar":
                    continue
                if src == n:
                    continue
                if e["known"].get(key, 0) < val:
                    e["known"][key] = val
                    waits.append((sem, val))
            if waits:
                e["prog"].append((waits, [], None, 0))

    def flush(self):
        nc = self.nc
        progs = {n: self.eng[n]["prog"] for n in ENGS}

        def replay(h, prog):
            for waits, fns, sem, inc in prog:
                for s, v in waits:
                    h.wait_ge(s, v)
                ins = None
                for f in fns:
                    ins = f(h)
                if ins is not None and sem is not None:
                    ins.then_inc(sem, inc)

        with nc.Block() as block:
            @block.sync
            def _(h):
                replay(h, progs["sync"])

            @block.gpsimd
            def _(h):
                replay(h, progs["gpsimd"])

            @block.scalar
            def _(h):
                replay(h, progs["scalar"])

            @block.vector
            def _(h):
                replay(h, progs["vector"])

            @block.tensor
            def _(h):
                replay(h, progs["tensor"])

        for n in ENGS:
            self.eng[n]["prog"] = []

    def close(self):
        self.es.close()

import math
from contextlib import ExitStack
import numpy as np
import concourse.bass as bass
import concourse.mybir as mybir

F32 = mybir.dt.float32
BF16 = mybir.dt.bfloat16
AF = mybir.ActivationFunctionType
ALU = mybir.AluOpType
AX = mybir.AxisListType

S = 2048
D = 4096
NT = S // 128
KC = D // 128
ALPHA = 4.0 ** 0.25
LN_EPS = 1e-5
NEG = -1e30
QSCALE = 1.0 / math.sqrt(128.0)
NEXP = 16384
NEC = NEXP // 128


class T:
    def __init__(self, h, name=""):
        self.h = h
        self.res = Res(name)

    def __getitem__(self, k):
        return self.h[k]


class Ctx:
    def __init__(self, nc):
        self.nc = nc
        self.fw = FW(nc)
        self.n = 0

    def sb(self, es, shape, dt, name):
        self.n += 1
        return T(es.enter_context(self.nc.sbuf_tensor(f"{name}_{self.n}", list(shape), dt)), name)

    def psum(self, es):
        self.n += 1
        hf = es.enter_context(self.nc.psum_tensor(f"psf_{self.n}", [128, 6 * 512], F32))
        hb = es.enter_context(self.nc.psum_tensor(f"psb_{self.n}", [128, 2 * 1024], BF16))
        psb = [hf[:, i * 512:(i + 1) * 512] for i in range(6)] + [hb[:, i * 1024:(i + 1) * 1024] for i in range(2)]
        psres = [Res(f"ps{i}") for i in range(8)]
        return psb, psres


def evac_copy(fw, i, out, in_, reads, writes, scale=None):
    if i % 2 == 0:
        if scale is None:
            fw.op("scalar", lambda h: h.copy(out=out, in_=in_), reads, writes)
        else:
            fw.op("scalar", lambda h: h.mul(out=out, in_=in_, mul=scale), reads, writes)
    else:
        if scale is None:
            fw.op("vector", lambda h: h.tensor_copy(out=out, in_=in_), reads, writes)
        else:
            fw.op("vector", lambda h: h.tensor_scalar(out=out, in0=in_, scalar1=scale, scalar2=None, op0=ALU.mult),
                  reads, writes)


def load_consts(cx, es, c_ident):
    fw = cx.fw
    identf = cx.sb(es, [128, 128], F32, "identf")
    identb = cx.sb(es, [128, 128], BF16, "identb")
    fw.dma("sync", identf[:], c_ident, writes=[identf.res])
    fw.op("vector", lambda h: h.tensor_copy(out=identb[:], in_=identf[:]), [identf.res], [identb.res])
    return identf, identb


def build_HT(cx, Hsrc, Hres, HT, tile0, ntiles, identb, psb, psres, xs, xb, dw=D, r=1):
    fw = cx.fw
    i = 0
    for tt in range(ntiles):
        r0 = (tile0 + tt) * 128
        for d0 in range(0, D, dw):
            b = i % len(xs)
            i += 1
            fw.dma("sync", xs[b][:], Hsrc[r0:r0 + 128, d0:d0 + dw], reads=[Hres], writes=[xs[b].res])
            fw.op("vector", lambda h, b=b: h.tensor_copy(out=xb[b][:], in_=xs[b][:]), [xs[b].res], [xb[b].res])
            for grp in range(dw // 1024):
                pb = grp % 2
                pt = psb[pb]
                kc0 = d0 // 128 + grp * 8
                fns = []
                for j in range(8):
                    fns.append(lambda h, j=j, grp=grp, pt=pt, b=b: h.transpose(
                        out=pt[:, j * 128:(j + 1) * 128], in_=xb[b][:, (grp * 8 + j) * 128:(grp * 8 + j + 1) * 128],
                        identity=identb[:]))
                fw.group("tensor", fns, [xb[b].res, identb.res], [psres[pb]])
                if r == 1:
                    evac_copy(fw, grp, HT[:, kc0:kc0 + 8, tt * 128:(tt + 1) * 128],
                              pt[:, 0:1024].rearrange("p (a b) -> p a b", a=8), [psres[pb]], [HT.res])
                else:
                    w = 128 // r
                    dst = HT[:, kc0:kc0 + 8, :].rearrange("p a (rho m) -> p a rho m", rho=r)[:, :, :, tt * w:(tt + 1) * w]
                    src = pt[:, 0:1024].rearrange("p (a jm jr) -> p a jr jm", a=8, jr=r)
                    evac_copy(fw, grp, dst, src, [psres[pb]], [HT.res])


def ht_chunk(HT, kc, r, tc):
    if r == 1:
        return HT[:, kc, tc * 512:(tc + 1) * 512]
    v = HT[:, kc, :].rearrange("p (m r) -> p r m", r=r)
    if r == 4:
        return v[:, tc, :]
    return v[:, 4 * tc:4 * tc + 4, :]


def proj_stage(cx, X, Xres, w, blocks, c_ident, gate=None):
    nc, fw = cx.nc, cx.fw
    with ExitStack() as es:
        identf, identb = load_consts(cx, es, c_ident)
        psb, psres = cx.psum(es)
        HT = cx.sb(es, [128, KC, S], BF16, "HT")
        xs = [cx.sb(es, [128, 2048], F32, "xs") for _ in range(2)]
        xb = [cx.sb(es, [128, 2048], BF16, "xb") for _ in range(2)]
        cur_r = None
        Wt = [cx.sb(es, [128, KC, 128], BF16, "Wt") for _ in range(3)]
        stgb = [cx.sb(es, [128, S], BF16, "stgb") for _ in range(2)]
        stgf = [cx.sb(es, [128, S], F32, "stgf") for _ in range(2)]
        ev = 0
        for bi, (col0, width, r, dst, dst_res, scale, dt) in enumerate(blocks):
            if r != cur_r:
                build_HT(cx, X, Xres, HT, 0, NT, identb, psb[6:8], psres[6:8], xs, xb, dw=2048, r=r)
                cur_r = r
            wt = Wt[bi % 3]
            stg = (stgb if dt == BF16 else stgf)[bi % 2]
            fw.dma("gpsimd", wt[:, :, 0:width], w[:, col0:col0 + width].rearrange("(kc p) c -> p kc c", p=128),
                   writes=[wt.res])
            for tc in range(4):
                pb = tc % 4
                outp = psb[pb][0:width, :]
                fns = [lambda h, kc=kc, tc=tc, outp=outp, wt=wt, r=r, width=width: h.matmul(
                    outp, lhsT=wt[:, kc, 0:width], rhs=HT[:, kc, tc * 512:(tc + 1) * 512],
                    start=(kc == 0), stop=(kc == KC - 1)) for kc in range(KC)]
                fw.group("tensor", fns, [wt.res, HT.res], [psres[pb]])
                evac_copy(fw, ev, stg[0:width, tc * 512:(tc + 1) * 512], psb[pb][0:width, :], [psres[pb]], [stg.res],
                          scale=scale)
                ev += 1
            fw.dma("sync", dst, stg[0:width, :], reads=[stg.res], writes=[dst_res])
        if gate is not None:
            col0, fbias, negc_d, negc_res = gate
            wt = Wt[0]
            fw.dma("gpsimd", wt[:, :, 0:32], w[:, col0:col0 + 32].rearrange("(kc p) c -> p kc c", p=128),
                   writes=[wt.res])
            fb = cx.sb(es, [32, 2], F32, "fb")
            fw.dma("sync", fb[:, 0:1], fbias.rearrange("(h o) -> h o", o=1), writes=[fb.res])
            fw.op("vector", lambda h: h.tensor_scalar(out=fb[:, 1:2], in0=fb[:, 0:1], scalar1=-1.0, scalar2=None,
                                                      op0=ALU.mult), [fb.res], [fb.res])
            sp = stgf[0]
            ngc = stgf[1]
            one = cx.sb(es, [32, 1], F32, "one")
            fw.op("vector", lambda h: h.memset(one[:], 1.0), [], [one.res])
            for tc in range(4):
                pb = tc % 4
                fns = [lambda h, kc=kc, tc=tc, pb=pb: h.matmul(
                    psb[pb][0:32, :], lhsT=wt[:, kc, 0:32], rhs=HT[:, kc, tc * 512:(tc + 1) * 512],
                    start=(kc == 0), stop=(kc == KC - 1)) for kc in range(KC)]
                fw.group("tensor", fns, [wt.res, HT.res], [psres[pb]])
                fw.op("scalar", lambda h, tc=tc, pb=pb: h.activation(
                    out=sp[0:32, tc * 512:(tc + 1) * 512], in_=psb[pb][0:32, :], func=AF.Exp, bias=fb[:, 1:2], scale=-1.0),
                    [psres[pb], fb.res], [sp.res])
            fw.op("scalar", lambda h: h.activation(out=sp[0:32, :], in_=sp[0:32, :], func=AF.Ln, bias=1.0, scale=1.0),
                  [sp.res], [sp.res])
            fw.op("vector", lambda h: h.tensor_tensor_scan(
                out=ngc[0:32, :], data0=one[:, 0:1].to_broadcast([32, S]), data1=sp[0:32, :], initial=0.0,
                op0=ALU.mult, op1=ALU.add), [one.res, sp.res], [ngc.res])
            fw.dma("sync", negc_d, ngc[0:32, :], reads=[ngc.res], writes=[negc_res])
        fw.barrier()
        fw.flush()


def attn_stage(cx, PROJ, PROJ_res, mode, att, att_res, consts, negc_d=None, negc_res=None, side=None):
    nc, fw = cx.nc, cx.fw
    c_ident, c_dist, c_mask, c_caus = consts
    with ExitStack() as es:
        identf, identb = load_consts(cx, es, c_ident)
        psb, psres = cx.psum(es)
        QKV = [[cx.sb(es, [128, S], BF16, "QKV") for _ in range(3)] for _ in range(2)]
        Vt = [cx.sb(es, [128, NT, 128], BF16, "Vt") for _ in range(2)]
        if mode == "dil":
            dist = cx.sb(es, [128, 256], F32, "dist")
            mask = cx.sb(es, [128, 256], F32, "mask")
            fw.dma("sync", dist[:], c_dist, writes=[dist.res])
            fw.dma("sync", mask[:], c_mask, writes=[mask.res])
            bias = [cx.sb(es, [128, 256], F32, "bias") for _ in range(2)]
            NKMAX = 256
            heads = [(g, h) for g in range(3) for h in range(16)]
        else:
            caus = cx.sb(es, [128, 128], F32, "caus")
            fw.dma("sync", caus[:], c_caus, writes=[caus.res])
            NKMAX = S
            heads = [(0, h) for h in range(32)]
            negcb = [cx.sb(es, [128, S], F32, "negcb") for _ in range(2)]
        Sb = [cx.sb(es, [128, NKMAX], F32, "Sb") for _ in range(2)]
        Pb = [cx.sb(es, [128, NKMAX], BF16, "Pb") for _ in range(2)]
        PT = [cx.sb(es, [128, NKMAX // 128, 128], BF16, "PT") for _ in range(2)]
        OUT = [cx.sb(es, [128, 136], F32, "OUT") for _ in range(4)]

        side_it = side(es, identb, psb, psres) if side is not None else iter(())
        side_per_head = -(-NEC // len(heads))
        for idx, (g, hh) in enumerate(heads):
            b = idx % 2
            r = (1, 4, 16)[g] if mode == "dil" else 1
            QT, KT, VT = QKV[b]
            for s in range(3):
                blk = g * 48 + s * 16 + hh if mode == "dil" else s * 32 + hh
                fw.dma("sync", QKV[b][s][:], PROJ[blk], reads=[PROJ_res], writes=[QKV[b][s].res])
            for half in range(2):
                fns = [lambda h, j=j, half=half, VT=VT: h.transpose(
                    out=psb[6][:, j * 128:(j + 1) * 128], in_=VT[:, (half * 8 + j) * 128:(half * 8 + j + 1) * 128],
                    identity=identb[:]) for j in range(8)]
                fw.group("tensor", fns, [VT.res, identb.res], [psres[6]])
                evac_copy(fw, half, Vt[b][:, half * 8:half * 8 + 8, :],
                          psb[6][:, 0:1024].rearrange("p (a b) -> p a b", a=8), [psres[6]], [Vt[b].res])
            if mode == "dil":
                slope = 2.0 ** (-8.0 * (hh + 1.0) / 16.0)
                fw.op("vector", lambda h, b=b, slope=slope, r=r: h.scalar_tensor_tensor(
                    out=bias[b][:], in0=dist[:], scalar=-slope * r, in1=mask[:], op0=ALU.mult, op1=ALU.add),
                    [dist.res, mask.res], [bias[b].res])
            else:
                fw.dma("sync", negcb[b][:], negc_d[hh].partition_broadcast(128), reads=[negc_res], writes=[negcb[b].res])
            def blk_params(qi):
                if mode == "dil":
                    nb = (S // r) // 128
                    rho, blk = qi // nb, qi % nb
                    k0 = qi - 1 if blk > 0 else qi
                    nk = (qi + 1 - k0) * 128
                    boff = 0 if blk > 0 else 128
                    return rho, blk, k0, nk, boff
                return 0, 0, 0, (qi + 1) * 128, 0

            def front(qi):
                sbi = qi % 2
                rho, blk, k0, nk, boff = blk_params(qi)
                ot = OUT[qi % 4]
                nchunk = (nk + 511) // 512
                for c in range(nchunk):
                    n0 = c * 512
                    n1 = min(nk, n0 + 512)
                    bk = (qi + c) % 4
                    fw.op("tensor", lambda h, qi=qi, n0=n0, n1=n1, bk=bk, k0=k0, QT=QT, KT=KT: h.matmul(
                        psb[bk][:, 0:n1 - n0], lhsT=QT[:, qi * 128:(qi + 1) * 128],
                        rhs=KT[:, k0 * 128 + n0:k0 * 128 + n1], start=True, stop=True),
                        [QT.res, KT.res], [psres[bk]])
                    if mode == "dil":
                        fw.op("vector", lambda h, bk=bk, n1=n1, sbi=sbi, boff=boff, b=b: h.tensor_tensor(
                            out=Sb[sbi][:, 0:n1], in0=psb[bk][:, 0:n1], in1=bias[b][:, boff:boff + n1], op=ALU.add),
                            [psres[bk], bias[b].res], [Sb[sbi].res])
                    else:
                        fw.op("vector", lambda h, bk=bk, n0=n0, n1=n1, sbi=sbi, b=b: h.tensor_tensor(
                            out=Sb[sbi][:, n0:n1], in0=psb[bk][:, 0:n1 - n0], in1=negcb[b][:, n0:n1], op=ALU.add),
                            [psres[bk], negcb[b].res], [Sb[sbi].res])
                if mode == "fox":
                    fw.op("vector", lambda h, sbi=sbi, nk=nk: h.tensor_tensor(
                        out=Sb[sbi][:, nk - 128:nk], in0=Sb[sbi][:, nk - 128:nk], in1=caus[:], op=ALU.add),
                        [Sb[sbi].res, caus.res], [Sb[sbi].res])
                fw.op("vector", lambda h, sbi=sbi, nk=nk, ot=ot: h.reduce_max(out=ot[:, 128:129], in_=Sb[sbi][:, 0:nk], axis=AX.X),
                      [Sb[sbi].res], [ot.res])
                fw.op("vector", lambda h, ot=ot: h.tensor_scalar(out=ot[:, 130:131], in0=ot[:, 128:129], scalar1=-1.0,
                                                                 scalar2=None, op0=ALU.mult), [ot.res], [ot.res])
                fw.op("scalar", lambda h, sbi=sbi, nk=nk, ot=ot: h.activation(
                    out=Pb[sbi][:, 0:nk], in_=Sb[sbi][:, 0:nk], func=AF.Exp, bias=ot[:, 130:131], scale=1.0,
                    accum_out=ot[:, 129:130]), [Sb[sbi].res, ot.res], [Pb[sbi].res, ot.res])

            def back(qi):
                sbi = qi % 2
                rho, blk, k0, nk, boff = blk_params(qi)
                ot = OUT[qi % 4]
                nkt = nk // 128
                for c0 in range(0, nkt, 8):
                    c1 = min(nkt, c0 + 8)
                    pbk = 6 + (c0 // 8) % 2
                    fns = [lambda h, j=j, sbi=sbi, pbk=pbk: h.transpose(
                        out=psb[pbk][:, (j % 8) * 128:(j % 8 + 1) * 128], in_=Pb[sbi][:, j * 128:(j + 1) * 128],
                        identity=identb[:]) for j in range(c0, c1)]
                    fw.group("tensor", fns, [Pb[sbi].res, identb.res], [psres[pbk]])
                    evac_copy(fw, qi + c0 // 8, PT[sbi][:, c0:c1, :],
                              psb[pbk][:, 0:(c1 - c0) * 128].rearrange("p (a b) -> p a b", a=c1 - c0),
                              [psres[pbk]], [PT[sbi].res])
                ob = 4 + qi % 2
                fns = [lambda h, j=j, sbi=sbi, k0=k0, nkt=nkt, ob=ob, b=b: h.matmul(
                    psb[ob][:, 0:128], lhsT=PT[sbi][:, j, :], rhs=Vt[b][:, k0 + j, :], start=(j == 0), stop=(j == nkt - 1))
                    for j in range(nkt)]
                fw.group("tensor", fns, [PT[sbi].res, Vt[b].res], [psres[ob]])
                fw.op("vector", lambda h, ot=ot: h.reciprocal(out=ot[:, 131:132], in_=ot[:, 129:130]), [ot.res], [ot.res])
                fw.op("vector", lambda h, ot=ot, ob=ob: h.tensor_scalar(
                    out=ot[:, 0:128], in0=psb[ob][:, 0:128], scalar1=ot[:, 131:132], scalar2=None, op0=ALU.mult),
                    [psres[ob], ot.res], [ot.res])
                if mode == "dil":
                    dst = att[g].rearrange("(m r) h c -> r m h c", r=r)[rho, blk * 128:(blk + 1) * 128, hh, 0:130]
                    fw.dma("sync", dst, ot[:, 0:130], reads=[ot.res], writes=[att_res])
                else:
                    fw.dma("sync", att[qi * 128:(qi + 1) * 128, hh * 128:(hh + 1) * 128], ot[:, 0:128],
                           reads=[ot.res], writes=[att_res])


            front(0)
            for qi in range(NT):
                if qi % 4 == 0:
                    next(side_it, None)
                if qi + 1 < NT:
                    front(qi + 1)
                back(qi)
        for _ in side_it:
            pass
        fw.barrier()
        fw.flush()


def layer_norm_tile(cx, Y, Yres_t, gB, bB, st, mv):
    fw = cx.fw
    fns = [lambda h, c=c: h.bn_stats(out=st[:, c, :], in_=Y[:, c * 512:(c + 1) * 512]) for c in range(8)]
    fw.group("vector", fns, [Yres_t], [st.res])
    fw.op("vector", lambda h: h.bn_aggr(out=mv[:, 0:2], in_=st[:].rearrange("p a b -> p (a b)")), [st.res], [mv.res])
    fw.op("vector", lambda h: h.tensor_scalar(out=mv[:, 2:3], in0=mv[:, 1:2], scalar1=LN_EPS, scalar2=None,
                                              op0=ALU.add), [mv.res], [mv.res])
    fw.op("scalar", lambda h: h.sqrt(out=mv[:, 3:4], in_=mv[:, 2:3]), [mv.res], [mv.res])
    fw.op("vector", lambda h: h.reciprocal(out=mv[:, 2:3], in_=mv[:, 3:4]), [mv.res], [mv.res])
    fw.op("vector", lambda h: h.tensor_scalar(out=mv[:, 3:4], in0=mv[:, 0:1], scalar1=mv[:, 2:3], scalar2=-1.0,
                                              op0=ALU.mult, op1=ALU.mult), [mv.res], [mv.res])
    fw.op("scalar", lambda h: h.activation(out=Y, in_=Y, func=AF.Identity, bias=mv[:, 3:4], scale=mv[:, 2:3]),
          [Yres_t, mv.res], [Yres_t])
    fw.op("vector", lambda h: h.tensor_tensor(out=Y, in0=Y, in1=gB[:], op=ALU.mult), [Yres_t, gB.res], [Yres_t])
    fw.op("vector", lambda h: h.tensor_tensor(out=Y, in0=Y, in1=bB[:], op=ALU.add), [Yres_t, bB.res], [Yres_t])


def ln_stage(cx, Ypre, Ypre_res, ln_g, ln_b, Hout, Hout_res):
    nc, fw = cx.nc, cx.fw
    with ExitStack() as es:
        gB = cx.sb(es, [128, D], F32, "gB")
        bB = cx.sb(es, [128, D], F32, "bB")
        fw.dma("sync", gB[:], ln_g.partition_broadcast(128), writes=[gB.res])
        fw.dma("sync", bB[:], ln_b.partition_broadcast(128), writes=[bB.res])
        Y = [cx.sb(es, [128, D], F32, "Y") for _ in range(3)]
        st = cx.sb(es, [128, 8, 6], F32, "st")
        mv = cx.sb(es, [128, 4], F32, "mv")
        for tile in range(NT):
            y = Y[tile % 3]
            fw.dma("sync", y[:], Ypre[tile * 128:(tile + 1) * 128, :], reads=[Ypre_res], writes=[y.res])
            layer_norm_tile(cx, y[:], y.res, gB, bB, st, mv)
            fw.dma("sync", Hout[tile * 128:(tile + 1) * 128, :], y[:], reads=[y.res], writes=[Hout_res])
        fw.barrier()
        fw.flush()


def mixout_stage(cx, X, Xres, att, att_res, w_out, KO, Ypre, Ypre_res, mode, c_ident):
    nc, fw = cx.nc, cx.fw
    KOC = KO // 128
    TG = 8 if mode == "dil" else 4
    with ExitStack() as es:
        identf, identb = load_consts(cx, es, c_ident)
        psb, psres = cx.psum(es)
        if mode == "dil":
            A = [cx.sb(es, [128, 16, 132], F32, "A") for _ in range(3)]
            wg = cx.sb(es, [128, 4, 16], F32, "wg")
            tmp = cx.sb(es, [128, 16, 128], F32, "tmp")
        else:
            A = [cx.sb(es, [128, KO], F32, "A") for _ in range(2)]
        mg = [cx.sb(es, [128, KO], BF16, "mg") for _ in range(2)]
        MT = cx.sb(es, [128, KOC, TG * 128], BF16, "MT")
        Wo = [cx.sb(es, [128, KOC, 512], BF16, "Wo") for _ in range(2)]
        Xc = [cx.sb(es, [128, TG, 512], F32, "Xc") for _ in range(2)]
        wcount = 0
        for tg in range(NT // TG):
            for tt in range(TG):
                tile = tg * TG + tt
                r0 = tile * 128
                if mode == "dil":
                    for g in range(3):
                        fw.dma("sync", A[g][:], att[g, r0:r0 + 128, :, :], reads=[att_res], writes=[A[g].res])
                    ares = [A[g].res for g in range(3)]
                    for g in range(3):
                        fw.op("scalar", lambda h, g=g: h.activation(out=A[g][:, :, 129], in_=A[g][:, :, 129], func=AF.Ln),
                              [A[g].res], [A[g].res])
                        fw.op("vector", lambda h, g=g: h.tensor_tensor(out=A[g][:, :, 128], in0=A[g][:, :, 128],
                                                                       in1=A[g][:, :, 129], op=ALU.add),
                              [A[g].res], [A[g].res])
                    lse = [A[g][:, :, 128] for g in range(3)]
                    fw.op("vector", lambda h, lse=lse: h.tensor_tensor(out=wg[:, 3, :], in0=lse[0], in1=lse[1], op=ALU.max),
                          ares, [wg.res])
                    fw.op("vector", lambda h, lse=lse: h.tensor_tensor(out=wg[:, 3, :], in0=wg[:, 3, :], in1=lse[2], op=ALU.max),
                          ares + [wg.res], [wg.res])
                    for g in range(3):
                        fw.op("vector", lambda h, g=g, lse=lse: h.tensor_tensor(out=wg[:, g, :], in0=lse[g], in1=wg[:, 3, :],
                                                                                op=ALU.subtract), ares + [wg.res], [wg.res])
                    fw.op("scalar", lambda h: h.activation(out=wg[:, 0:3, :], in_=wg[:, 0:3, :], func=AF.Exp),
                          [wg.res], [wg.res])
                    fw.op("vector", lambda h: h.tensor_tensor(out=wg[:, 3, :], in0=wg[:, 0, :], in1=wg[:, 1, :], op=ALU.add),
                          [wg.res], [wg.res])
                    fw.op("vector", lambda h: h.tensor_tensor(out=wg[:, 3, :], in0=wg[:, 3, :], in1=wg[:, 2, :], op=ALU.add),
                          [wg.res], [wg.res])
                    fw.op("vector", lambda h: h.reciprocal(out=wg[:, 3, :], in_=wg[:, 3, :]), [wg.res], [wg.res])
                    for g in range(3):
                        fw.op("vector", lambda h, g=g: h.tensor_tensor(out=wg[:, g, :], in0=wg[:, g, :], in1=wg[:, 3, :],
                                                                       op=ALU.mult), [wg.res], [wg.res])
                    m = mg[tile % 2]
                    mv3 = m[:].rearrange("p (h c) -> p h c", c=128)
                    fw.op("vector", lambda h: h.tensor_tensor(out=tmp[:], in0=A[0][:, :, 0:128],
                                                              in1=wg[:, 0, :].unsqueeze(2).to_broadcast([128, 16, 128]),
                                                              op=ALU.mult), [A[0].res, wg.res], [tmp.res])
                    for g in (1, 2):
                        fw.op("vector", lambda h, g=g: h.tensor_tensor(
                            out=A[g][:, :, 0:128], in0=A[g][:, :, 0:128],
                            in1=wg[:, g, :].unsqueeze(2).to_broadcast([128, 16, 128]), op=ALU.mult),
                            [A[g].res, wg.res], [A[g].res])
                    fw.op("vector", lambda h: h.tensor_tensor(out=tmp[:], in0=tmp[:], in1=A[1][:, :, 0:128], op=ALU.add),
                          [tmp.res, A[1].res], [tmp.res])
                    fw.op("vector", lambda h, mv3=mv3: h.tensor_tensor(out=mv3, in0=tmp[:], in1=A[2][:, :, 0:128], op=ALU.add),
                          [tmp.res, A[2].res], [m.res])
                else:
                    a = A[tile % 2]
                    m = mg[tile % 2]
                    fw.dma("sync", a[:], att[r0:r0 + 128, :], reads=[att_res], writes=[a.res])
                    fw.op("scalar", lambda h, a=a, m=m: h.copy(out=m[:], in_=a[:]), [a.res], [m.res])
                for c0 in range(0, KOC, 8):
                    pbk = 6 + (c0 // 8) % 2
                    fns = [lambda h, j=j, m=m, pbk=pbk: h.transpose(
                        out=psb[pbk][:, (j % 8) * 128:(j % 8 + 1) * 128], in_=m[:, j * 128:(j + 1) * 128],
                        identity=identb[:]) for j in range(c0, c0 + 8)]
                    fw.group("tensor", fns, [m.res, identb.res], [psres[pbk]])
                    evac_copy(fw, c0 // 8, MT[:, c0:c0 + 8, tt * 128:(tt + 1) * 128],
                              psb[pbk][:, 0:1024].rearrange("p (a b) -> p a b", a=8), [psres[pbk]], [MT.res])
            for dc in range(D // 512):
                wb = wcount % 2
                wcount += 1
                fw.dma("gpsimd", Wo[wb][:], w_out[:, dc * 512:(dc + 1) * 512].rearrange("(kc p) c -> p kc c", p=128),
                       writes=[Wo[wb].res])
                fw.dma("sync", Xc[wb][:], X[tg * TG * 128:(tg + 1) * TG * 128, dc * 512:(dc + 1) * 512]
                       .rearrange("(t p) c -> p t c", p=128), reads=[Xres], writes=[Xc[wb].res])
                for tt in range(TG):
                    pb = (dc * TG + tt) % 6
                    fns = [lambda h, kc=kc, tt=tt, pb=pb, wb=wb: h.matmul(
                        psb[pb], lhsT=MT[:, kc, tt * 128:(tt + 1) * 128], rhs=Wo[wb][:, kc, :],
                        start=(kc == 0), stop=(kc == KOC - 1)) for kc in range(KOC)]
                    fw.group("tensor", fns, [MT.res, Wo[wb].res], [psres[pb]])
                    fw.op("vector", lambda h, tt=tt, pb=pb, wb=wb: h.scalar_tensor_tensor(
                        out=Xc[wb][:, tt, :], in0=Xc[wb][:, tt, :], scalar=ALPHA, in1=psb[pb],
                        op0=ALU.mult, op1=ALU.add), [Xc[wb].res, psres[pb]], [Xc[wb].res])
                fw.dma("sync", Ypre[tg * TG * 128:(tg + 1) * TG * 128, dc * 512:(dc + 1) * 512]
                       .rearrange("(t p) c -> p t c", p=128), Xc[wb][:], reads=[Xc[wb].res], writes=[Ypre_res])
        fw.barrier()
        fw.flush()


def prep_gen(cx, es, u, v, UTs, UTs_res, Vs, Vs_res, identb, psb, psres):
    fw = cx.fw
    NBUF = 3
    Ub = [cx.sb(es, [128, D], BF16, "Ub") for _ in range(NBUF)]
    UT = [cx.sb(es, [128, KC, 128], BF16, "UTp") for _ in range(2)]
    Vb = [cx.sb(es, [128, D], BF16, "Vbp") for _ in range(NBUF)]

    def load(E):
        fw.dma("gpsimd", Ub[E % NBUF][:], u[E * 128:(E + 1) * 128, :], writes=[Ub[E % NBUF].res], max_dma_last_dim=8192)
        fw.dma("gpsimd", Vb[E % NBUF][:], v[E * 128:(E + 1) * 128, :], writes=[Vb[E % NBUF].res], max_dma_last_dim=8192)

    def compute(E):
        ub, ut, vb = Ub[E % NBUF], UT[E % 2], Vb[E % NBUF]
        for grp in range(KC // 8):
            pbk = 6 + grp % 2
            fns = [lambda h, j=j, grp=grp, ub=ub, pbk=pbk: h.transpose(
                out=psb[pbk][:, j * 128:(j + 1) * 128], in_=ub[:, (grp * 8 + j) * 128:(grp * 8 + j + 1) * 128],
                identity=identb[:]) for j in range(8)]
            fw.group("tensor", fns, [ub.res, identb.res], [psres[pbk]])
            evac_copy(fw, grp, ut[:, grp * 8:grp * 8 + 8, :],
                      psb[pbk][:, 0:1024].rearrange("p (a b) -> p a b", a=8), [psres[pbk]], [ut.res])
        fw.dma("sync", UTs[E], ut[:], reads=[ut.res], writes=[UTs_res])
        fw.dma("sync", Vs[:, :, E, :].rearrange("c e d -> e c d"), vb[:].rearrange("e (c d) -> e c d", c=8),
               reads=[vb.res], writes=[Vs_res])

    load(0)
    load(1)
    for E in range(NEC):
        compute(E)
        if E + 2 < NEC:
            load(E + 2)
        yield E


def prep_experts(cx, u, v, UTs, UTs_res, Vs, Vs_res, c_ident):
    nc, fw = cx.nc, cx.fw
    with ExitStack() as es:
        identf, identb = load_consts(cx, es, c_ident)
        psb, psres = cx.psum(es)
        for _ in prep_gen(cx, es, u, v, UTs, UTs_res, Vs, Vs_res, identb, psb, psres):
            pass
        fw.barrier()
        fw.flush()


def peer_stage(cx, Hin, Hin_res, PQ, PQ_res, keys, UTs, UTs_res, Vs, Vs_res, Ypre, Ypre_res, c_ident):
    nc, fw = cx.nc, cx.fw
    TG = 2
    TW = TG * 128
    NIB = NEC // 4
    with ExitStack() as es:
        identf, identb = load_consts(cx, es, c_ident)
        psb, psres = cx.psum(es)
        kraw = cx.sb(es, [128, 2, 128], F32, "kraw")
        kT = cx.sb(es, [128, 2, 128], F32, "kT")
        for half in range(2):
            fw.dma("sync", kraw[:, half, :], keys[half], writes=[kraw.res])
        for half in range(2):
            fw.op("tensor", lambda h, half=half: h.transpose(out=psb[half][:, 0:128], in_=kraw[:, half, :], identity=identf[:]),
                  [kraw.res, identf.res], [psres[half]])
            fw.op("vector", lambda h, half=half: h.tensor_copy(out=kT[:, half, :], in_=psb[half][:, 0:128]),
                  [psres[half]], [kT.res])
        xs = [cx.sb(es, [128, 2048], F32, "xs")]
        xb = [cx.sb(es, [128, 2048], BF16, "xb")]
        HTg = cx.sb(es, [128, KC, TW], BF16, "HTg")
        QTt = cx.sb(es, [128, 16, 128], F32, "QTt")
        Ssb = [cx.sb(es, [128, 16, 128], F32, "Ssb") for _ in range(TG)]
        thr = cx.sb(es, [128, TG, 8], F32, "thr")
        bg = cx.sb(es, [128, TG, 8], F32, "bg")
        T16 = cx.sb(es, [128, 2, 16], F32, "T16")
        wk = cx.sb(es, [128, 128], F32, "wk")
        cand = cx.sb(es, [128, 256], F32, "cand")
        cand2 = cx.sb(es, [128, 256], F32, "cand2")
        C16 = cx.sb(es, [128, 16], F32, "C16")
        E16 = cx.sb(es, [128, 16], F32, "E16")
        sc = cx.sb(es, [128, 4], F32, "sc")
        NSB, NGB = 3, 8
        sumt = [cx.sb(es, [128, 4, 128], F32, "sumt") for _ in range(NSB)]
        et = [cx.sb(es, [128, 4, 128], BF16, "et") for _ in range(NSB)]
        gt = [cx.sb(es, [128, 4, 128], BF16, "gt") for _ in range(NGB)]
        UT = [cx.sb(es, [128, KC, 128], BF16, "UT") for _ in range(3)]
        Vb = [cx.sb(es, [128, 8, 512], BF16, "Vb") for _ in range(4)]
        act = [cx.sb(es, [128, TW], F32, "act") for _ in range(2)]
        ACTT = cx.sb(es, [128, NEC, TW], BF16, "ACTT")
        xc = [cx.sb(es, [128, 512], F32, "xc") for _ in range(2)]
        cnt = dict(ut=0, vb=0, g=0, xc=0)

        NG = NT // TG

        def prologue(tg):
            t0 = tg * TW
            build_HT(cx, Hin, Hin_res, HTg, tg * TG, TG, identb, psb[6:8], psres[6:8], xs, xb, dw=2048)
            for tt in range(TG):
                fw.dma("sync", QTt[:], PQ[:, :, t0 + tt * 128:t0 + (tt + 1) * 128].rearrange("a p t -> p a t"),
                       reads=[PQ_res], writes=[QTt.res])
                for q4 in range(4):
                    fns = [lambda h, q4=q4, j=j: h.matmul(
                        psb[q4][:, j * 128:(j + 1) * 128], lhsT=QTt[:, q4 * 4 + j, :], rhs=kT[:, (q4 * 4 + j) % 2, :],
                        start=True, stop=True) for j in range(4)]
                    fw.group("tensor", fns, [QTt.res, kT.res], [psres[q4]])
                    evac_copy(fw, q4, Ssb[tt][:, q4 * 4:q4 * 4 + 4, :], psb[q4].rearrange("p (a b) -> p a b", a=4),
                              [psres[q4]], [Ssb[tt].res])
                for hd in range(8):
                    for half in range(2):
                        src = Ssb[tt][:, 2 * hd + half, :]
                        fw.op("vector", lambda h, src=src, half=half: h.max(out=T16[:, half, 0:8], in_=src),
                              [Ssb[tt].res], [T16.res])
                        fw.op("vector", lambda h, src=src, half=half: h.match_replace(
                            out=wk[:], in_to_replace=T16[:, half, 0:8], in_values=src, imm_value=NEG),
                            [Ssb[tt].res, T16.res], [wk.res])
                        fw.op("vector", lambda h, half=half: h.max(out=T16[:, half, 8:16], in_=wk[:]),
                              [wk.res], [T16.res])
                    fw.op("vector", lambda h: h.tensor_tensor(
                        out=cand[:].rearrange("p (a b) -> p a b", a=16),
                        in0=T16[:, 0, :].unsqueeze(2).to_broadcast([128, 16, 16]),
                        in1=T16[:, 1, :].unsqueeze(1).to_broadcast([128, 16, 16]), op=ALU.add),
                        [T16.res], [cand.res])
                    fw.op("vector", lambda h: h.max(out=C16[:, 0:8], in_=cand[:]), [cand.res], [C16.res])
                    fw.op("vector", lambda h: h.match_replace(out=cand2[:], in_to_replace=C16[:, 0:8], in_values=cand[:],
                                                              imm_value=NEG), [cand.res, C16.res], [cand2.res])
                    fw.op("vector", lambda h: h.max(out=C16[:, 8:16], in_=cand2[:]), [cand2.res], [C16.res])
                    fw.op("vector", lambda h: h.tensor_scalar(out=sc[:, 0:1], in0=C16[:, 0:1], scalar1=-1.0, scalar2=None,
                                                              op0=ALU.mult), [C16.res], [sc.res])
                    fw.op("scalar", lambda h: h.activation(out=E16[:], in_=C16[:], func=AF.Exp, bias=sc[:, 0:1], scale=1.0,
                                                           accum_out=sc[:, 1:2]), [C16.res, sc.res], [E16.res, sc.res])
                    fw.op("scalar", lambda h: h.activation(out=sc[:, 2:3], in_=sc[:, 1:2], func=AF.Ln), [sc.res], [sc.res])
                    fw.op("vector", lambda h, tt=tt, hd=hd: h.tensor_tensor(out=bg[:, tt, hd:hd + 1], in0=sc[:, 0:1],
                                                                           in1=sc[:, 2:3], op=ALU.subtract),
                          [sc.res], [bg.res])
                    fw.op("vector", lambda h, tt=tt, hd=hd: h.tensor_copy(out=thr[:, tt, hd:hd + 1], in_=C16[:, 15:16]),
                          [C16.res], [thr.res])


        def phase1(tg):
            gq = {}

            def g_front(ib, k):
                tt, hd = k // 8, k % 8
                sbi = cnt["g"] % NSB
                gb = cnt["g"] % NGB
                cnt["g"] += 1
                gq[(ib, k)] = gb
                s0 = Ssb[tt][:, 2 * hd, ib * 4:(ib + 1) * 4]
                s1 = Ssb[tt][:, 2 * hd + 1, :]
                fw.op("gpsimd", lambda h: h.tensor_tensor(
                    out=sumt[sbi][:], in0=s0.unsqueeze(2).to_broadcast([128, 4, 128]),
                    in1=s1.unsqueeze(1).to_broadcast([128, 4, 128]), op=ALU.add),
                    [Ssb[tt].res], [sumt[sbi].res])
                fw.op("scalar", lambda h: h.activation(
                    out=et[sbi][:], in_=sumt[sbi][:], func=AF.Exp, bias=bg[:, tt, hd:hd + 1], scale=1.0),
                    [sumt[sbi].res, bg.res], [et[sbi].res])
                fw.op("vector", lambda h: h.scalar_tensor_tensor(
                    out=gt[gb][:], in0=sumt[sbi][:], scalar=thr[:, tt, hd:hd + 1], in1=et[sbi][:],
                    op0=ALU.is_ge, op1=ALU.mult), [sumt[sbi].res, thr.res, et[sbi].res], [gt[gb].res])

            def g_back(ib, k):
                par = ib % 2
                tt, hd = k // 8, k % 8
                gb = gq.pop((ib, k))
                fns = [lambda h, ec=ec: h.matmul(
                    psb[2 * par + ec // 2][:, (ec % 2) * TW + tt * 128:(ec % 2) * TW + (tt + 1) * 128],
                    lhsT=gt[gb][:, ec, :], rhs=identb[:], start=(k == 0 and ec % 2 == 0),
                    stop=(k == 8 * TG - 1 and ec % 2 == 1), skip_group_check=True) for ec in range(4)]
                fw.group("tensor", fns, [gt[gb].res, identb.res], [psres[2 * par], psres[2 * par + 1]])

            def st_front(E):
                ut = UT[cnt["ut"] % 3]
                cnt["ut"] += 1
                fw.dma("sync", ut[:], UTs[E], reads=[UTs_res], writes=[ut.res])
                sbk = 4 + E % 2
                fns = [lambda h, kc=kc: h.matmul(
                    psb[sbk][:, 0:TW], lhsT=ut[:, kc, :], rhs=HTg[:, kc, :], start=(kc == 0), stop=(kc == KC - 1))
                    for kc in range(KC)]
                fw.group("tensor", fns, [ut.res, HTg.res], [psres[sbk]])

            def st_gelu(E):
                sbk = 4 + E % 2
                a = act[E % 2]
                fw.op("scalar", lambda h: h.activation(out=a[:], in_=psb[sbk][:, 0:TW], func=AF.Gelu),
                      [psres[sbk]], [a.res])

            def st_mult(E):
                ib, ec = E // 4, E % 4
                a = act[E % 2]
                gbank = 2 * (ib % 2) + ec // 2
                fw.op("vector", lambda h: h.tensor_tensor(
                    out=ACTT[:, E, :], in0=a[:], in1=psb[gbank][:, (ec % 2) * TW:(ec % 2 + 1) * TW],
                    op=ALU.mult), [a.res, psres[gbank]], [ACTT.res])

            NK = 8 * TG
            for k0 in range(0, NK, NGB):
                for k in range(k0, k0 + NGB):
                    g_front(0, k)
                for k in range(k0, k0 + NGB):
                    g_back(0, k)
            for ib in range(NIB):
                for pr in range(2):
                    ks = range(pr * (NK // 2), (pr + 1) * (NK // 2))
                    nxt = ib + 1 < NIB
                    if nxt:
                        for k in ks:
                            g_front(ib + 1, k)
                    E0 = ib * 4 + pr * 2
                    st_front(E0)
                    if nxt:
                        for k in list(ks)[:len(ks) // 2]:
                            g_back(ib + 1, k)
                    st_front(E0 + 1)
                    if nxt:
                        for k in list(ks)[len(ks) // 2:]:
                            g_back(ib + 1, k)
                    st_gelu(E0)
                    st_gelu(E0 + 1)
                    st_mult(E0)
                    st_mult(E0 + 1)


        def phase2(tg):
            for dc in range(8):
                if dc == 4 and tg + 1 < NG:
                    prologue(tg + 1)
                banks = [(2 * dc) % 6, (2 * dc + 1) % 6]
                for eb in range(NEC // 8):
                    vb = Vb[cnt["vb"] % 4]
                    cnt["vb"] += 1
                    fw.dma("sync", vb[:], Vs[dc, :, eb * 8:(eb + 1) * 8, :], reads=[Vs_res], writes=[vb.res])
                    fns = []
                    for c in range(8):
                        E = eb * 8 + c
                        for tt in range(TG):
                            fns.append(lambda h, E=E, c=c, tt=tt, vb=vb: h.matmul(
                                psb[banks[tt]], lhsT=ACTT[:, E, tt * 128:(tt + 1) * 128], rhs=vb[:, c, :],
                                start=(E == 0), stop=(E == NEC - 1)))
                    fw.group("tensor", fns, [ACTT.res, vb.res], [psres[banks[0]], psres[banks[1]]])
                for tt in range(TG):
                    tile = tg * TG + tt
                    x_ = xc[cnt["xc"] % 2]
                    cnt["xc"] += 1
                    fw.dma("sync", x_[:], Hin[tile * 128:(tile + 1) * 128, dc * 512:(dc + 1) * 512],
                           reads=[Hin_res], writes=[x_.res])
                    fw.op("vector", lambda h, x_=x_, tt=tt: h.scalar_tensor_tensor(
                        out=x_[:], in0=x_[:], scalar=ALPHA, in1=psb[banks[tt]], op0=ALU.mult, op1=ALU.add),
                        [x_.res, psres[banks[tt]]], [x_.res])
                    fw.dma("sync", Ypre[tile * 128:(tile + 1) * 128, dc * 512:(dc + 1) * 512], x_[:],
                           reads=[x_.res], writes=[Ypre_res])


        prologue(0)
        for tg in range(NG):
            phase1(tg)
            phase2(tg)
        fw.barrier()
        fw.flush()

import numpy as np
import concourse.bass as bass
import concourse.mybir as mybir
from concourse.bass_utils import run_bass_kernel_spmd

ALL_STAGES = ("proj0", "attn0", "mix0", "lnm0", "pq0", "prep0", "peer0", "lnf0",
              "proj1", "attn1", "mix1", "lnm1", "pq1", "prep1", "peer1", "lnf1")


def make_consts():
    ident = np.eye(128, dtype=np.float32)
    i = np.arange(128)[:, None]
    j = np.arange(256)[None, :]
    dist = (128 + i - j).astype(np.float32)
    mask = np.where((dist >= 0) & (dist <= 128), 0.0, NEG).astype(np.float32)
    jj = np.arange(128)[None, :]
    caus = np.where(jj <= i, 0.0, NEG).astype(np.float32)
    return {"c_ident": ident, "c_dist": dist, "c_mask": mask, "c_caus": caus}


def build(stages=ALL_STAGES, dbg_out=()):
    nc = bass.Bass("TRN2", target_bir_lowering=False)
    cx = Ctx(nc)

    def din(name, shape, dt=F32):
        return nc.dram_tensor(name, list(shape), dt, kind="ExternalInput").ap()

    def dscr(name, shape, dt=F32):
        kind = "ExternalOutput" if name in dbg_out else "Internal"
        return nc.dram_tensor(name, list(shape), dt, kind=kind).ap()

    SHAPES = {"x": [S, D], "a_w_in": [D, 18432], "a_w_out": [2048, D], "b_w_in": [D, 12320], "b_f_bias": [32],
              "b_w_out": [D, D], "peer_w_q": [2, D, 2048], "peer_sub_keys": [2, 2, 128, 128],
              "peer_u": [2, NEXP, D], "peer_v": [2, NEXP, D], "ln_mix_g": [2, D], "ln_mix_b": [2, D],
              "ln_ffn_g": [2, D], "ln_ffn_b": [2, D], "c_ident": [128, 128], "c_dist": [128, 256],
              "c_mask": [128, 256], "c_caus": [128, 128]}
    cache = {}

    def I(name):
        if name not in cache:
            cache[name] = din(name, SHAPES[name])
        return cache[name]
    c_ident, c_dist, c_mask, c_caus = I("c_ident"), I("c_dist"), I("c_mask"), I("c_caus")
    consts = (c_ident, c_dist, c_mask, c_caus)

    PROJ = dscr("PROJ", [144, 128, S], BF16)
    ATT0 = dscr("ATT0", [3, S, 16, 132])
    ATT1 = dscr("ATT1", [S, D])
    YPRE = dscr("YPRE", [S, D])
    H1 = dscr("H1", [S, D])
    H2 = dscr("H2", [S, D])
    H3 = dscr("H3", [S, D])
    PQ = dscr("PQ", [16, 128, S])
    NEGC = dscr("NEGC", [32, S])
    UTs = dscr("UTs", [NEC, 128, KC, 128], BF16)
    Vs = dscr("Vs", [8, 128, NEC, 512], BF16)
    y = nc.dram_tensor("y", [S, D], F32, kind="ExternalOutput").ap()

    R = lambda: Res("d")

    def layer(i, hin, hmid, hout):
        if f"proj{i}" in stages:
            if i == 0:
                blocks = []
                for g in range(3):
                    for s in range(3):
                        for h in range(16):
                            blocks.append((g * 6144 + s * 2048 + h * 128, 128, (1, 4, 16)[g],
                                           PROJ[g * 48 + s * 16 + h], R(), QSCALE if s == 0 else None, BF16))
                proj_stage(cx, hin, R(), I("a_w_in"), blocks, c_ident)
            else:
                blocks = []
                for s in range(3):
                    for h in range(32):
                        blocks.append((s * D + h * 128, 128, 1, PROJ[s * 32 + h], R(), QSCALE if s == 0 else None, BF16))
                proj_stage(cx, hin, R(), I("b_w_in"), blocks, c_ident, gate=(3 * D, I("b_f_bias"), NEGC, R()))
        if f"attn{i}" in stages:
            side = None
            if f"prep{i}" in stages:
                ur, vr = R(), R()
                side = lambda es, identb, psb, psres: prep_gen(cx, es, I("peer_u")[i], I("peer_v")[i], UTs, ur, Vs, vr,
                                                               identb, psb, psres)
            if i == 0:
                attn_stage(cx, PROJ, R(), "dil", ATT0, R(), consts, side=side)
            else:
                attn_stage(cx, PROJ, R(), "fox", ATT1, R(), consts, NEGC, R(), side=side)
        if f"mix{i}" in stages:
            if i == 0:
                mixout_stage(cx, hin, R(), ATT0, R(), I("a_w_out"), 2048, YPRE, R(), "dil", c_ident)
            else:
                mixout_stage(cx, hin, R(), ATT1, R(), I("b_w_out"), 4096, YPRE, R(), "fox", c_ident)
        if f"lnm{i}" in stages:
            ln_stage(cx, YPRE, R(), I("ln_mix_g")[i], I("ln_mix_b")[i], hmid, R())
        if f"pq{i}" in stages:
            blocks = [(hp * 128, 128, 1, PQ[hp], R(), None, F32) for hp in range(16)]
            proj_stage(cx, hmid, R(), I("peer_w_q")[i], blocks, c_ident)
        if f"prep{i}" in stages and f"attn{i}" not in stages:
            prep_experts(cx, I("peer_u")[i], I("peer_v")[i], UTs, R(), Vs, R(), c_ident)
        if f"peer{i}" in stages:
            peer_stage(cx, hmid, R(), PQ, R(), I("peer_sub_keys")[i], UTs, R(), Vs, R(), YPRE, R(), c_ident)
        if f"lnf{i}" in stages:
            ln_stage(cx, YPRE, R(), I("ln_ffn_g")[i], I("ln_ffn_b")[i], hout, R())

    layer(0, I("x"), H1, H2)
    layer(1, H2, H3, y)
    print("instructions:", cx.fw.ninstr, flush=True)
    nc._used_inputs = list(cache)
    return nc


WEIGHTS = ("a_w_in", "a_w_out", "b_w_in", "b_f_bias", "b_w_out", "peer_w_q", "peer_sub_keys", "peer_u", "peer_v",
           "ln_mix_g", "ln_mix_b", "ln_ffn_g", "ln_ffn_b")


def make_in_maps(inputs, n):
    consts = make_consts()
    shared = {}
    for k in WEIGHTS:
        a = np.ascontiguousarray(inputs[k])
        if k in ("a_w_in", "a_w_out", "b_w_in", "b_f_bias", "b_w_out"):
            a = a[0]
        shared[k] = a
    shared.update(consts)
    maps = []
    for c in range(n):
        m = dict(shared)
        m["x"] = np.ascontiguousarray(inputs["x"][c])
        maps.append(m)
    return maps


def kernel(**inputs):
    n = 8
    nc = build()
    in_maps = make_in_maps(inputs, n)
    res = run_bass_kernel_spmd(nc, in_maps, core_ids=list(range(n)))
    return np.stack([np.asarray(r["y"]) for r in res.results], axis=0).astype(np.float32)
```
